# Optimizing a Trainium2 kernel written in Bass

```python
import math
import jax, jax.numpy as jnp
from jax import lax
import numpy as np

D_MODEL = 2048
BATCH = 1
SEQ = 8192
DEPTH = 4

HEAD_DIM = 128
GRID_W = 64
NA_HEADS = 8
NA_WIDTH = NA_HEADS * HEAD_DIM
NA_WIN_R = 8
NA_WIN_C = 16
WA_HEADS = 8
WA_KV_HEADS = 2
WA_WIDTH = WA_HEADS * HEAD_DIM
WA_KV_WIDTH = WA_KV_HEADS * HEAD_DIM
WA_WINDOW = 128
WA_BLOCK = 128
MIX_WIDTH = NA_WIDTH + WA_WIDTH
PROJ_SPLITS = (NA_WIDTH, NA_WIDTH, NA_WIDTH, WA_WIDTH, WA_KV_WIDTH, WA_KV_WIDTH)
PROJ_WIDTH = sum(PROJ_SPLITS)
D_FF = 5632
CONV_W = 3
ROPE_THETA = 10000.0
EPS = 1e-6
NEG = -1e30

kernel_name = "hybrid_na_swa_convffn_encoder"


def rms_norm(x, g):
    xf = x.astype(jnp.float32)
    y = xf * lax.rsqrt(jnp.mean(xf * xf, axis=-1, keepdims=True) + EPS)
    return (y * g.astype(jnp.float32)).astype(x.dtype)


def rope(x, positions):
    d = x.shape[-1]
    inv = ROPE_THETA ** (-jnp.arange(0, d, 2, dtype=jnp.float32) / d)
    ang = positions.astype(jnp.float32)[:, None] * inv[None, :]
    cos = jnp.cos(ang)[None, :, None, :]
    sin = jnp.sin(ang)[None, :, None, :]
    xf = x.astype(jnp.float32)
    x1, x2 = xf[..., : d // 2], xf[..., d // 2:]
    out = jnp.concatenate([x1 * cos - x2 * sin, x2 * cos + x1 * sin], axis=-1)
    return out.astype(x.dtype)


def neighborhood_attention(q, k, v, rpb):
    b, s, h, d = q.shape
    rows = s // GRID_W
    wr = min(NA_WIN_R, rows)
    wc = NA_WIN_C
    qg = q.reshape(b, rows, GRID_W, h, d)
    kg = k.reshape(b, rows, GRID_W, h, d)
    vg = v.reshape(b, rows, GRID_W, h, d)
    r = jnp.arange(rows)
    row_start = jnp.clip(r - wr // 2, 0, rows - wr)
    row_idx = row_start[:, None] + jnp.arange(wr)[None, :]
    k_rows = kg[:, row_idx]
    v_rows = vg[:, row_idx]
    c = jnp.arange(GRID_W)
    col_start = jnp.clip(c - wc // 2, 0, GRID_W - wc)
    col_mask = (c[None, :] >= col_start[:, None]) & (c[None, :] < col_start[:, None] + wc)
    dr = row_idx - r[:, None] + (NA_WIN_R - 1)
    dc = jnp.clip(c[None, :] - c[:, None], -(wc - 1), wc - 1) + (NA_WIN_C - 1)
    bias = rpb[:, dr]
    bias = bias[:, :, :, dc]
    bias = jnp.transpose(bias, (0, 1, 3, 2, 4)).astype(jnp.float32)
    scale = 1.0 / math.sqrt(d)
    sc = jnp.einsum('brqhd,brwkhd->bhrqwk', qg, k_rows).astype(jnp.float32) * scale
    sc = sc + bias[None]
    sc = jnp.where(col_mask[:, None, :], sc, NEG)
    shp = sc.shape
    p = jax.nn.softmax(sc.reshape(shp[:-2] + (wr * GRID_W,)), axis=-1).reshape(shp)
    out = jnp.einsum('bhrqwk,brwkhd->brqhd', p.astype(v.dtype), v_rows)
    return out.reshape(b, s, h * d)


def windowed_gqa_sink(q, k, v, sink):
    b, s, hq, d = q.shape
    hkv = k.shape[2]
    g = hq // hkv
    nb = s // WA_BLOCK
    qb = q.reshape(b, nb, WA_BLOCK, hkv, g, d)
    pad = ((0, 0), (WA_BLOCK, WA_BLOCK), (0, 0), (0, 0))
    kp = jnp.pad(k, pad).reshape(b, nb + 2, WA_BLOCK, hkv, d)
    vp = jnp.pad(v, pad).reshape(b, nb + 2, WA_BLOCK, hkv, d)
    kw = jnp.concatenate([kp[:, :-2], kp[:, 1:-1], kp[:, 2:]], axis=2)
    vw = jnp.concatenate([vp[:, :-2], vp[:, 1:-1], vp[:, 2:]], axis=2)
    blk = jnp.arange(nb)[:, None]
    qpos = blk * WA_BLOCK + jnp.arange(WA_BLOCK)[None, :]
    kpos = (blk - 1) * WA_BLOCK + jnp.arange(3 * WA_BLOCK)[None, :]
    diff = kpos[:, None, :] - qpos[:, :, None]
    valid = (jnp.abs(diff) <= WA_WINDOW) & (kpos[:, None, :] >= 0) & (kpos[:, None, :] < s)
    scale = 1.0 / math.sqrt(d)
    sc = jnp.einsum('bnqhgd,bnkhd->bhgnqk', qb, kw).astype(jnp.float32) * scale
    sc = jnp.where(valid, sc, NEG)
    sink_l = sink.astype(jnp.float32).reshape(hkv, g)[None, :, :, None, None, None]
    m = jnp.maximum(jnp.max(sc, axis=-1, keepdims=True), sink_l)
    e = jnp.exp(sc - m)
    p = e / (jnp.sum(e, axis=-1, keepdims=True) + jnp.exp(sink_l - m))
    out = jnp.einsum('bhgnqk,bnkhd->bnqhgd', p.astype(v.dtype), vw)
    return out.reshape(b, s, hq * d)


def depthwise_conv(u, w, bias):
    up = jnp.pad(u, ((0, 0), (1, 1), (0, 0)))
    return up[:, :-2] * w[0] + up[:, 1:-1] * w[1] + up[:, 2:] * w[2] + bias


def setup_inputs(seed: int = 0) -> dict:
    key = jax.random.key(seed)
    ks = jax.random.split(key, 20)
    f32 = jnp.float32
    nrm = lambda k, shp, sc: jax.random.normal(k, shp, f32) * sc
    centre = jnp.zeros((CONV_W, 1), f32).at[CONV_W // 2].set(1.0)
    return {
        "x": nrm(ks[0], (BATCH, SEQ, D_MODEL), 1.0),
        "positions": jnp.arange(SEQ, dtype=jnp.int32),
        "ln1_g": 1.0 + nrm(ks[1], (DEPTH, D_MODEL), 0.02),
        "w_in": nrm(ks[2], (DEPTH, D_MODEL, PROJ_WIDTH), D_MODEL ** -0.5),
        "qn_a": 1.0 + nrm(ks[3], (DEPTH, HEAD_DIM), 0.02),
        "kn_a": 1.0 + nrm(ks[4], (DEPTH, HEAD_DIM), 0.02),
        "rpb": nrm(ks[5], (DEPTH, NA_HEADS, 2 * NA_WIN_R - 1, 2 * NA_WIN_C - 1), 0.1),
        "qn_b": 1.0 + nrm(ks[6], (DEPTH, HEAD_DIM), 0.02),
        "kn_b": 1.0 + nrm(ks[7], (DEPTH, HEAD_DIM), 0.02),
        "sink": nrm(ks[8], (DEPTH, WA_HEADS), 0.5),
        "on_a": 1.0 + nrm(ks[9], (DEPTH, NA_WIDTH), 0.02),
        "on_b": 1.0 + nrm(ks[10], (DEPTH, WA_WIDTH), 0.02),
        "w_out": nrm(ks[11], (DEPTH, MIX_WIDTH, D_MODEL), 0.5 * MIX_WIDTH ** -0.5),
        "ln2_g": 1.0 + nrm(ks[12], (DEPTH, D_MODEL), 0.02),
        "w_up": nrm(ks[13], (DEPTH, D_MODEL, 2 * D_FF), D_MODEL ** -0.5),
        "conv_w": centre[None] + nrm(ks[14], (DEPTH, CONV_W, 2 * D_FF), 0.3),
        "conv_b": nrm(ks[15], (DEPTH, 2 * D_FF), 0.01),
        "w_down": nrm(ks[16], (DEPTH, D_FF, D_MODEL), 0.5 * D_FF ** -0.5),
    }


def reference(x, positions, ln1_g, w_in, qn_a, kn_a, rpb, qn_b, kn_b, sink,
              on_a, on_b, w_out, ln2_g, w_up, conv_w, conv_b, w_down):
    b, s, _ = x.shape
    cuts = list(np.cumsum(PROJ_SPLITS)[:-1])
    for l in range(DEPTH):
        h = rms_norm(x, ln1_g[l])
        proj = h @ w_in[l]
        qa, ka, va, qb, kb, vb = jnp.split(proj, cuts, axis=-1)
        qa = rms_norm(qa.reshape(b, s, NA_HEADS, HEAD_DIM), qn_a[l])
        ka = rms_norm(ka.reshape(b, s, NA_HEADS, HEAD_DIM), kn_a[l])
        va = va.reshape(b, s, NA_HEADS, HEAD_DIM)
        oa = neighborhood_attention(qa, ka, va, rpb[l])
        qb = rope(rms_norm(qb.reshape(b, s, WA_HEADS, HEAD_DIM), qn_b[l]), positions)
        kb = rope(rms_norm(kb.reshape(b, s, WA_KV_HEADS, HEAD_DIM), kn_b[l]), positions)
        vb = vb.reshape(b, s, WA_KV_HEADS, HEAD_DIM)
        ob = windowed_gqa_sink(qb, kb, vb, sink[l])
        o = jnp.concatenate([rms_norm(oa, on_a[l]), rms_norm(ob, on_b[l])], axis=-1)
        x = x + o @ w_out[l]
        h2 = rms_norm(x, ln2_g[l])
        u = depthwise_conv(h2 @ w_up[l], conv_w[l], conv_b[l])
        gate, up = u[..., :D_FF], u[..., D_FF:]
        x = x + (jax.nn.silu(gate) * up) @ w_down[l]
    return x
```

```python
import math
import numpy as np
from contextlib import ExitStack

import concourse.bass as bass
import concourse.mybir as mybir
from concourse.bass_utils import run_bass_kernel_spmd

F32 = mybir.dt.float32
BF16 = mybir.dt.bfloat16
I32 = mybir.dt.int32
AF = mybir.ActivationFunctionType
ALU = mybir.AluOpType

NC = 8
T = 1024
D = 2048
KC = 16
DFF = 5632
FC = 44
PW = 4608
EPS = 1e-6
NVL = 412
NEG = -30000.0
TH = [(0, 342), (342, 684), (684, 1024)]
GROUPS = [(0, 12), (12, 12), (24, 12), (36, 8)]


class Tok:
    __slots__ = ("sem", "val", "eng", "key")

    def __init__(self, sem, val, eng, key):
        self.sem, self.val, self.eng, self.key = sem, val, eng, key


class Slot:
    def __init__(self, sem, name):
        self.sem, self.cnt, self.name = sem, 0, name


class Tracker:
    def __init__(self, nc):
        self.nc = nc
        self.eng = {"pe": nc.tensor, "act": nc.scalar, "dve": nc.vector, "pool": nc.gpsimd, "sp": nc.sync}
        self.sem, self.cnt, self.semkey = {}, {}, {}
        self.seen = {e: {} for e in self.eng}
        self.res = {}
        self.pending = {e: [] for e in self.eng}
        self.epoch = 0
        self.rots = {}
        self.roti = {}
        self.all_sems = []
        for e, n in (("sp", 12), ("pool", 6), ("act", 4)):
            self.rots[e] = [Slot(nc.alloc_semaphore(f"rot_{e}_{i}"), f"rot_{e}_{i}") for i in range(n)]
            self.all_sems += [sl.sem for sl in self.rots[e]]
            self.roti[e] = 0
        self.new_epoch()

    def new_epoch(self):
        for e in self.eng:
            assert not self.pending[e], e
            self.semkey[e] = f"s_{e}_{self.epoch}"
            self.sem[e] = self.nc.alloc_semaphore(self.semkey[e])
            self.all_sems.append(self.sem[e])
            self.cnt[e] = 0
        self.epoch += 1

    def _wait(self, e, tok):
        if tok is None:
            return
        if tok.eng == e and e == "pe":
            return
        assert tok.val is not None, "dependency on an unsignalled instruction"
        if self.seen[e].get(tok.key, 0) >= tok.val:
            return
        self.eng[e].wait_ge(tok.sem, tok.val)
        self.seen[e][tok.key] = tok.val

    def rot(self, e):
        sl = self.rots[e][self.roti[e] % len(self.rots[e])]
        self.roti[e] += 1
        if sl.cnt:
            self._wait(e, Tok(sl.sem, sl.cnt, "dma", sl.name))
        return sl

    def op(self, e, fn, reads=(), writes=(), sig=True, dma=False):
        for r in reads:
            st = self.res.get(r)
            if st:
                self._wait(e, st[0])
        for w in writes:
            st = self.res.get(w)
            if st:
                self._wait(e, st[0])
                for t in st[1].values():
                    self._wait(e, t)
        slot = self.rot(e) if dma else None
        ins = fn()
        if dma:
            slot.cnt += 16
            ins.then_inc(slot.sem, 16)
            tok = Tok(slot.sem, slot.cnt, "dma", slot.name)
        elif sig:
            self.cnt[e] += 1
            ins.then_inc(self.sem[e], 1)
            tok = Tok(self.sem[e], self.cnt[e], e, self.semkey[e])
            for p in self.pending[e]:
                p.sem, p.val, p.key = tok.sem, tok.val, tok.key
            self.pending[e] = []
        else:
            tok = Tok(None, None, e, None)
            self.pending[e].append(tok)
        for r in reads:
            st = self.res.setdefault(r, [None, {}])
            st[1][(tok.eng, id(tok)) if tok.key is None else tok.key] = tok
        for w in writes:
            self.res[w] = [tok, {}]
        return tok

    def alias(self, src, dst):
        toks = {}
        for n in list(src) + list(dst):
            st = self.res.get(n)
            if not st:
                continue
            for t in [st[0]] + list(st[1].values()):
                if t is None:
                    continue
                assert t.val is not None
                if t.key not in toks or toks[t.key].val < t.val:
                    toks[t.key] = t
        for n in dst:
            self.res[n] = [None, dict(toks)]

    def wait_all(self, e, names):
        for n in names:
            st = self.res.get(n)
            if st:
                self._wait(e, st[0])
                for t in st[1].values():
                    self._wait(e, t)


def _a_slots(b):
    s = [-4, -2, 0, 2, 4]
    if b == 0:
        s = s + [6]
    if b == 7:
        s = [-6] + s
    return s


def _a_mask_index():
    idx = {}
    n = 0
    for d in (-4, -2, 0, 2, 4):
        for b in (2, 3, 4, 5):
            idx[(b, d)] = n
        n += 1
    for b in (0, 1, 6, 7):
        for d in _a_slots(b):
            idx[(b, d)] = n
            n += 1
    return idx, n


A_MIDX, N_AMASK = _a_mask_index()


def _neg_a(core):
    out = np.zeros((128, N_AMASK, 128), np.float32)
    kk = np.arange(128)
    ki, kc = kk // 64, kk % 64
    qi, qc = kk // 64, kk % 64
    done = set()
    for (b, d), m in A_MIDX.items():
        if m in done:
            continue
        done.add(m)
        r0 = 16 * core + 2 * b
        kr = (r0 + d + ki)[:, None]
        qr = (r0 + qi)[None, :]
        rs = np.clip(qr - 4, 0, 120)
        cs = np.clip(qc - 8, 0, 48)[None, :]
        valid = (kr >= 0) & (kr < 128) & (kr >= rs) & (kr < rs + 8) & (kc[:, None] >= cs) & (kc[:, None] < cs + 16)
        out[:, m, :] = np.where(valid, 0.0, NEG)
    return out


def _neg_b(core):
    out = np.zeros((128, 4, 128), np.float32)
    kk = np.arange(128)[:, None]
    qq = np.arange(128)[None, :]
    g0 = np.where(kk >= qq, 0.0, NEG)
    g2 = np.where(kk <= qq, 0.0, NEG)
    out[:, 0, :] = NEG if core == 0 else g0
    out[:, 1, :] = g0
    out[:, 2, :] = g2
    out[:, 3, :] = NEG if core == NC - 1 else g2
    return out


def _rpb_table(rpb_l):
    p = np.arange(128)
    i, kc = p // 64, p % 64
    m = np.arange(14)
    qc = np.arange(64)
    dr = 6 - m[None, :, None] + i[:, None, None]
    dc = kc[:, None, None] - qc[None, None, :]
    ok = (np.abs(dr) <= 7) & (np.abs(dc) <= 15)
    dri = np.clip(dr + 7, 0, 14) + 0 * dc
    dci = np.clip(dc + 15, 0, 30) + 0 * dr
    g = rpb_l[:, dri, dci]
    g = np.where(ok[None], g, np.float32(0.0))
    return np.ascontiguousarray(g.transpose(1, 0, 2, 3).reshape(128, 8, 896)).astype(np.float32)


def _cmat():
    c = np.zeros((128, 385), np.float32)
    c[:, 0:128] = np.eye(128, dtype=np.float32)
    c[:, 128:256] = 1.0
    for m_ in range(128):
        if m_ < 64:
            c[m_ + 64, 256 + m_] = -1.0
        else:
            c[m_ - 64, 256 + m_] = 1.0
    j = np.arange(128) % 64
    c[:, 384] = (10000.0 ** (-(2.0 * j) / 128.0)).astype(np.float32)
    return c


V_LN1, V_LN2, V_ONA, V_ONB, V_QNA, V_KNA, V_QNB, V_KNB, V_CW, V_CB, V_SINK = 0, 16, 32, 40, 48, 49, 50, 51, 52, 316, 404


def _vecs(inp, layers):
    out = []
    for l in layers:
        cols = [inp["ln1_g"][l].reshape(16, 128).T, inp["ln2_g"][l].reshape(16, 128).T,
                inp["on_a"][l].reshape(8, 128).T, inp["on_b"][l].reshape(8, 128).T]
        for k in ("qn_a", "kn_a", "qn_b", "kn_b"):
            cols.append(inp[k][l].reshape(128, 1))
        cols.append(inp["conv_w"][l].reshape(3, 88, 128).transpose(2, 0, 1).reshape(128, 264))
        cols.append(inp["conv_b"][l].reshape(88, 128).T)
        cols.append(np.broadcast_to(inp["sink"][l][None, :], (128, 8)))
        out.append(np.concatenate([np.asarray(c, np.float32) for c in cols], axis=1))
    return np.ascontiguousarray(np.stack(out, axis=0))


U_KA, U_VA, U_BIAS, U_QT, U_HALO, U_END = 0, 3072, 6144, 9728, 13824, 18432
U_HG, U_CG = 0, 8256
ATT_NAMES = ["kA0", "kA1", "vA0", "vA1", "bias0", "bias1", "qT0", "qT1", "halo"]
FFN_NAMES = ["hg0", "hg1", "cg0", "cg1", "cu0", "cu1", "sg0", "sg1"]


class Prog:
    def __init__(self, n_layers):
        self.L = n_layers
        nc = self.nc = bass.Bass("TRN2", target_bir_lowering=False)
        L = n_layers
        dt = nc.dram_tensor
        self.xT_d = dt("xT", [D, T], F32, kind="ExternalInput").ap()
        self.pos_d = dt("pos", [T], I32, kind="ExternalInput").ap()
        self.vecs_d = dt("vecs", [L, 128, NVL], F32, kind="ExternalInput").ap()
        self.cmat_d = dt("cmat", [128, 385], F32, kind="ExternalInput").ap()
        self.negA_d = dt("negA", [128, N_AMASK, 128], F32, kind="ExternalInput").ap()
        self.negB_d = dt("negB", [128, 4, 128], F32, kind="ExternalInput").ap()
        self.rpbT_d = dt("rpbT", [L, 128, 8, 897], F32, kind="ExternalInput").ap()
        self.idx_d = dt("idx", [128, 2], I32, kind="ExternalInput").ap()
        self.flag_d = dt("flag", [128, 2], F32, kind="ExternalInput").ap()
        self.win_d = dt("w_in", [L, 256, PW], F32, kind="ExternalInput").ap()
        self.wout_d = dt("w_out", [L, 256, D], F32, kind="ExternalInput").ap()
        self.wup_d = dt("w_up", [L, 256, 2 * DFF], F32, kind="ExternalInput").ap()
        self.wdn_d = dt("w_down", [L, 704, D], F32, kind="ExternalInput").ap()
        self.yT_d = dt("yT", [D, T], F32, kind="ExternalOutput").ap()
        self.wb, self.wg = {}, {}
        for l in range(L):
            self.wb[("in", l, 0)] = dt(f"wb_in{l}", [256, PW], BF16).ap()
            self.wg[("in", l, 0)] = dt(f"wg_in{l}", [2048, PW], BF16).ap()
            self.wb[("out", l, 0)] = dt(f"wb_out{l}", [256, D], BF16).ap()
            self.wg[("out", l, 0)] = dt(f"wg_out{l}", [2048, D], BF16).ap()
            for q in range(4):
                self.wb[("up", l, q)] = dt(f"wb_up{l}_{q}", [256, 2816], BF16).ap()
                self.wg[("up", l, q)] = dt(f"wg_up{l}_{q}", [2048, 2816], BF16).ap()
            for q in range(2):
                self.wb[("dn", l, q)] = dt(f"wb_dn{l}_{q}", [704, 1024], BF16).ap()
                self.wg[("dn", l, q)] = dt(f"wg_dn{l}_{q}", [DFF, 1024], BF16).ap()
        self.kv_b = [dt(f"kv_b{i}", [256, PW], BF16).ap() for i in range(2)]
        self.kv_g = [dt(f"kv_g{i}", [2048, PW], BF16).ap() for i in range(2)]
        self.xb_b = [dt(f"xb_b{i}", [256, 16], F32).ap() for i in range(2)]
        self.xb_g = [dt(f"xb_g{i}", [2048, 16], F32).ap() for i in range(2)]
        dk = dict(kind="ExternalOutput") if DEBUG else {}
        self.ka_o = dt("ka_own", [8, 128, T], BF16, **dk).ap()
        self.va_o = dt("va_own", [8, 128, 1024], BF16, **dk).ap()
        self.kb_o = dt("kb_own", [2, 128, T], BF16, **dk).ap()
        self.vb_o = dt("vb_own", [8, 128, 256], BF16, **dk).ap()
        if DEBUG:
            self.dbg_ot = dt("dbg_ot", [2, 128, 16, 512], BF16, kind="ExternalOutput").ap()
            self.dbg_halo = dt("dbg_halo", [2, 128, PW], BF16, kind="ExternalOutput").ap()
            self.dbg_xmid = dt("dbg_xmid", [128, KC, T + 2], F32, kind="ExternalOutput").ap()
            self.dbg_hg = dt("dbg_hg", [128, 12, 344], BF16, kind="ExternalOutput").ap()
            self.dbg_h2 = dt("dbg_h2", [128, KC, 512], BF16, kind="ExternalOutput").ap()

        sb = nc.alloc_sbuf_tensor
        self.x = sb("x", [128, KC, T + 2], F32)
        self.vecs = sb("vecs_sb", [128, NVL], F32)
        self.cm32 = sb("cm32", [128, 385], F32)
        self.cmb = sb("cmb", [128, 256], BF16)
        self.negA = sb("negA_sb", [128, N_AMASK, 128], BF16)
        self.negB = sb("negB_sb", [128, 4, 128], BF16)
        self.idx = sb("idx_sb", [128, 2], I32)
        self.flag = sb("flag_sb", [128, 2], F32)
        self.cos = sb("cos_sb", [128, T], F32)
        self.sin = sb("sin_sb", [128, T], F32)
        self.sinkexp = sb("sinkexp", [128, 8], F32)
        self.hT = sb("hT", [128, KC, 512], BF16)
        self.OT = sb("OT", [128, 16, 512], BF16)
        self.U = sb("U", [128, U_END], BF16)
        self.ring = [sb(f"ring{i}", [128, KC, 256], BF16) for i in range(4)]
        self.sq = [sb(f"sq{i}", [128, 512], BF16) for i in range(2)]
        self.rt = sb("rt", [128, 512], F32)
        self.rstd = sb("rstd", [128, 512], F32)
        self.qn = sb("qn", [128, 512], F32)
        self.t1 = sb("t1", [128, 512], F32)
        self.t2 = sb("t2", [128, 512], F32)
        self.t3 = sb("t3", [128, 512], F32)
        self.kst = [sb(f"kst{i}", [128, 512], BF16) for i in range(2)]
        self.vst = [sb(f"vst{i}", [128, 256], BF16) for i in range(2)]
        self.pT = [sb(f"pT{i}", [128, 512], BF16) for i in range(3)]
        self.xbs = sb("xbs", [128, 2, 16], F32)
        self.xh = sb("xh", [128, 2, 16], F32)
        self.scr = sb("scr", [128, 8], F32)
        self.xsv = sb("xsv", [128, 2, 16], F32)
        self.xtmp = sb("xtmp", [128, 16], F32)
        U = self.U
        v3 = lambda a, n0, n1: U[:, a:a + n0 * n1].rearrange("p (h t) -> p h t", h=n0)
        self.kA = [v3(U_KA + i * 1536, 2, 768) for i in range(2)]
        self.vA = [v3(U_VA + i * 1536, 6, 256) for i in range(2)]
        self.bias = [v3(U_BIAS + i * 1792, 2, 896) for i in range(2)]
        self.qT = [v3(U_QT + i * 2048, 4, 512) for i in range(2)]
        self.halo = U[:, U_HALO:U_HALO + PW]
        self.hg = [v3(U_HG + i * 4128, 12, 344) for i in range(2)]
        f32v = lambda a: U[:, a:a + 688].bitcast(F32)
        self.cg = [f32v(U_CG + i * 688) for i in range(2)]
        self.cu = [f32v(U_CG + (2 + i) * 688) for i in range(2)]
        self.sg = [f32v(U_CG + (4 + i) * 688) for i in range(2)]
        ps = nc.alloc_psum_tensor
        self.ps_acc = [ps(f"ps_acc{i}", [128, 512], F32) for i in range(2)]
        self.ps_ss = ps("ps_ss", [128, 512], F32)
        self.ps_s = [ps(f"ps_s{i}", [128, 512], F32) for i in range(2)]
        self.ps_o = ps("ps_o", [128, 512], F32)
        self.ps_d = ps("ps_d", [128, 512], F32)
        self.ps_r = ps("ps_r", [128, 512], F32)
        self.tr = Tracker(nc)
        self.cc_sem = nc.alloc_semaphore("cc_sem")
        self.cc_cnt = 0
        self.cnt = {}
        self.plan = []
        self.plan_i = 0
        self.load_i = 0
        self.loaded = {}

    def rr(self, key, n):
        v = self.cnt.get(key, 0)
        self.cnt[key] = v + 1
        return v % n

    def vcol(self, off, n=1):
        return self.vecs[:, off:off + n]

    def mm(self, out, lhsT, rhs, start, stop, reads, writes, sig=None):
        sig = stop if sig is None else sig
        return self.tr.op("pe", lambda: self.nc.tensor.matmul(out, lhsT=lhsT, rhs=rhs, start=start, stop=stop),
                          reads=reads, writes=writes, sig=sig)

    def act(self, out, in_, func, reads, writes, **kw):
        return self.tr.op("act", lambda: self.nc.scalar.activation(out=out, in_=in_, func=func, **kw), reads=reads, writes=writes)

    def dve(self, fn, reads, writes):
        return self.tr.op("dve", fn, reads=reads, writes=writes)

    def dma(self, e, out, in_, reads, writes, **kw):
        return self.tr.op(e, lambda: self.tr.eng[e].dma_start(out=out, in_=in_, **kw), reads=reads, writes=writes, dma=True)

    def pool_signal(self, reads, writes):
        return self.tr.op("pool", lambda: self.nc.gpsimd.memset(self.scr[:, 0:1], 0.0), reads=list(reads),
                          writes=list(writes) + ["scr"])

    def allgather(self, src, dst, reads, writes):
        self.tr.wait_all("pool", list(reads) + list(writes))
        self.nc.gpsimd.collective_compute("AllGather", ALU.bypass, replica_groups=[list(range(NC))],
                                          ins=[src.opt()], outs=[dst.opt()]).then_inc(self.cc_sem)
        self.cc_cnt += 1
        self.nc.gpsimd.wait_ge(self.cc_sem, self.cc_cnt)
        self.pool_signal(reads, writes)

    def _emit_loads(self, upto):
        while self.load_i < min(upto, len(self.plan)):
            src, nk, res = self.plan[self.load_i]
            i = self.load_i % 4
            if res not in self.tr.res:
                break
            self.dma("sp", self.ring[i][:, 0:nk, :], src.rearrange("(k p) n -> p k n", p=128), [res], [f"ring{i}"])
            self.load_i += 1

    def wnext(self):
        self._emit_loads(self.plan_i + 3)
        assert self.load_i > self.plan_i, "weight tile consumed before its gather was emitted"
        i = self.plan_i % 4
        self.plan_i += 1
        return self.ring[i], f"ring{i}"

    def plan_layer(self, li):
        P = []
        win, wout = self.wg[("in", li, 0)], self.wg[("out", li, 0)]
        rin, rout = f"wg_in{li}_0", f"wg_out{li}_0"
        for j in range(2):
            for hp in range(4):
                P.append((win[:, 1024 + hp * 256:1024 + hp * 256 + 256], KC, rin))
            P.append((win[:, 4096:4352], KC, rin))
            for vt in range(5):
                c0 = 2048 + vt * 256 if vt < 4 else 4352
                P.append((win[:, c0:c0 + 256], KC, rin))
        for j in range(2):
            for hp in range(4):
                P.append((win[:, hp * 256:hp * 256 + 256], KC, rin))
            for g in range(2):
                for wp in range(2):
                    c0 = 3072 + (4 * g + 2 * wp) * 128
                    P.append((win[:, c0:c0 + 256], KC, rin))
            for np_ in range(8):
                P.append((wout[:, np_ * 256:np_ * 256 + 256], KC, rout))
        for ti in range(3):
            for (f0, nf) in GROUPS:
                for duo in range(nf // 2):
                    f = f0 + 2 * duo
                    q, qo = divmod(f * 128, 2816)
                    P.append((self.wg[("up", li, q)][:, qo:qo + 256], KC, f"wg_up{li}_{q}"))
                    q2, qo2 = divmod(DFF + f * 128, 2816)
                    P.append((self.wg[("up", li, q2)][:, qo2:qo2 + 256], KC, f"wg_up{li}_{q2}"))
                for np_ in range(8):
                    q, qo = divmod(np_ * 256, 1024)
                    P.append((self.wg[("dn", li, q)][f0 * 128:(f0 + nf) * 128, qo:qo + 256], nf, f"wg_dn{li}_{q}"))
        return P

    def gather_weight(self, kind, li, q):
        src = {"in": self.win_d, "out": self.wout_d, "up": self.wup_d, "dn": self.wdn_d}[kind]
        if kind == "up":
            s = src[li][:, q * 2816:(q + 1) * 2816]
        elif kind == "dn":
            s = src[li][:, q * 1024:(q + 1) * 1024]
        else:
            s = src[li]
        wb, wg = self.wb[(kind, li, q)], self.wg[(kind, li, q)]
        rb, rg = f"wb_{kind}{li}_{q}", f"wg_{kind}{li}_{q}"
        bsz = 256 if kind == "up" else 512
        self.dma("pool", wb.rearrange("r (a b) -> r a b", b=bsz), s.rearrange("r (a b) -> r a b", b=bsz), [], [rb])
        self.allgather(wb, wg, reads=[rb], writes=[rg])

    def prologue(self):
        nc = self.nc
        pi = math.pi
        self.dma("sp", self.x[:, :, 1:T + 1], self.xT_d.rearrange("(k p) t -> p k t", p=128), [], ["x"])
        self.dma("sp", self.cm32[:], self.cmat_d, [], ["cm32"])
        self.dma("sp", self.idx[:], self.idx_d, [], ["idx"])
        self.dma("sp", self.flag[:], self.flag_d, [], ["flag"])
        self.dma("pool", self.negA[:, 0:16, :], self.negA_d[:, 0:16, :], [], ["negA"])
        self.dma("pool", self.negA[:, 16:N_AMASK, :], self.negA_d[:, 16:N_AMASK, :], [], ["negA"])
        self.dma("pool", self.negB[:], self.negB_d, [], ["negB"])
        self.dve(lambda: nc.vector.tensor_copy(out=self.cmb[:], in_=self.cm32[:, 0:256]), ["cm32"], ["cmb"])
        self.dve(lambda: nc.vector.memset(self.x[:, :, 0:T + 2:T + 1], 0.0), [], ["xhalo"])
        invf = self.cm32[:, 384:385]
        posi = self.t1[:].bitcast(I32)
        ang, kf, r, pf = self.qn, self.t3, self.rt, self.rstd
        for hf in range(2):
            cs = slice(hf * 512, hf * 512 + 512)
            self.dma("sp", posi, self.pos_d[hf * 512:hf * 512 + 512].partition_broadcast(128), [], ["t1"])
            self.dve(lambda: nc.vector.tensor_copy(out=pf[:], in_=posi), ["t1"], ["rstd"])
            self.dve(lambda: nc.vector.tensor_scalar(out=ang[:], in0=pf[:], scalar1=invf, scalar2=None, op0=ALU.mult),
                     ["rstd", "cm32"], ["qn"])
            self.dve(lambda: nc.vector.tensor_scalar(out=posi, in0=ang[:], scalar1=1.0 / (2 * pi), scalar2=None, op0=ALU.mult),
                     ["qn"], ["t1"])
            self.dve(lambda: nc.vector.tensor_copy(out=kf[:], in_=posi), ["t1"], ["t3"])
            c1 = float(np.float32(2 * pi))
            c2 = float(2 * pi - c1)
            self.dve(lambda: nc.vector.scalar_tensor_tensor(out=r[:], in0=kf[:], scalar=-c1, in1=ang[:], op0=ALU.mult, op1=ALU.add),
                     ["t3", "qn"], ["rt"])
            self.dve(lambda: nc.vector.scalar_tensor_tensor(out=r[:], in0=kf[:], scalar=-c2, in1=r[:], op0=ALU.mult, op1=ALU.add),
                     ["t3", "rt"], ["rt"])
            self.dve(lambda: nc.vector.tensor_scalar(out=kf[:], in0=r[:], scalar1=pi, scalar2=-2 * pi, op0=ALU.is_gt, op1=ALU.mult), ["rt"], ["t3"])
            self.dve(lambda: nc.vector.tensor_tensor(out=r[:], in0=r[:], in1=kf[:], op=ALU.add), ["rt", "t3"], ["rt"])
            self.dve(lambda: nc.vector.tensor_scalar(out=kf[:], in0=r[:], scalar1=-pi, scalar2=2 * pi, op0=ALU.is_lt, op1=ALU.mult), ["rt"], ["t3"])
            self.dve(lambda: nc.vector.tensor_tensor(out=r[:], in0=r[:], in1=kf[:], op=ALU.add), ["rt", "t3"], ["rt"])
            self.act(self.sin[:, cs], r[:], AF.Sin, ["rt"], ["sin"])
            self.dve(lambda: nc.vector.tensor_scalar(out=ang[:], in0=r[:], scalar1=pi / 2, scalar2=None, op0=ALU.add), ["rt"], ["qn"])
            self.dve(lambda: nc.vector.tensor_scalar(out=kf[:], in0=ang[:], scalar1=pi, scalar2=-2 * pi, op0=ALU.is_gt, op1=ALU.mult), ["qn"], ["t3"])
            self.dve(lambda: nc.vector.tensor_tensor(out=ang[:], in0=ang[:], in1=kf[:], op=ALU.add), ["qn", "t3"], ["qn"])
            self.act(self.cos[:, cs], ang[:], AF.Sin, ["qn"], ["cos"])

    def sumsq(self, chunks, n, res_in):
        nch = len(chunks)
        for i, c in enumerate(chunks):
            b = self.rr("sq", 2)
            self.act(self.sq[b][:, 0:n], c, AF.Square, res_in, [f"sq{b}"])
            self.mm(self.ps_ss[:, 0:n], self.cmb[:, 128:256], self.sq[b][:, 0:n], i == 0, i == nch - 1, [f"sq{b}", "cmb"], ["ps_ss"], sig=True)

    def finish_rstd(self, n, scale, bias):
        nc = self.nc
        self.act(self.rt[:, 0:n], self.ps_ss[:, 0:n], AF.Sqrt, ["ps_ss"], ["rt"], scale=scale, bias=bias)
        self.dve(lambda: nc.vector.reciprocal(out=self.rstd[:, 0:n], in_=self.rt[:, 0:n]), ["rt"], ["rstd"])

    def make_h(self, c0, n, goff):
        nc = self.nc
        self.sumsq([self.x[:, kc, c0:c0 + n] for kc in range(KC)], n, ["x", "xhalo"])
        self.finish_rstd(n, 1.0 / D, EPS)
        for kc in range(KC):
            self.dve(lambda kc=kc: nc.vector.scalar_tensor_tensor(
                out=self.hT[:, kc, 0:n], in0=self.x[:, kc, c0:c0 + n], scalar=self.vcol(goff + kc),
                in1=self.rstd[:, 0:n], op0=ALU.mult, op1=ALU.mult), ["x", "xhalo", "rstd", "vecs"], ["hT"])

    def head_norm(self, ps, psn, gcol, out_ap, out_res, fold_scale, n=512):
        nc = self.nc
        b = self.rr("sq", 2)
        self.act(self.sq[b][:, 0:n], ps, AF.Square, [psn], [f"sq{b}"])
        self.mm(self.ps_ss[:, 0:n], self.cmb[:, 128:256], self.sq[b][:, 0:n], True, True, [f"sq{b}", "cmb"], ["ps_ss"])
        if fold_scale:
            self.finish_rstd(n, 1.0, 128.0 * EPS)
        else:
            self.finish_rstd(n, 1.0 / 128.0, EPS)
        self.dve(lambda: nc.vector.scalar_tensor_tensor(out=out_ap, in0=ps, scalar=self.vcol(gcol), in1=self.rstd[:, 0:n],
                                                        op0=ALU.mult, op1=ALU.mult), [psn, "rstd", "vecs"], out_res)

    def rope(self, tok0, out_ap, out_res):
        nc = self.nc
        cs = slice(tok0, tok0 + 512)
        self.mm(self.ps_r[:], self.cm32[:, 256:384], self.qn[:], True, True, ["qn", "cm32"], ["ps_r"])
        self.dve(lambda: nc.vector.tensor_tensor(out=self.t1[:], in0=self.qn[:], in1=self.cos[:, cs], op=ALU.mult), ["qn", "cos"], ["t1"])
        self.dve(lambda: nc.vector.tensor_tensor(out=self.t2[:], in0=self.ps_r[:], in1=self.sin[:, cs], op=ALU.mult), ["ps_r", "sin"], ["t2"])
        self.dve(lambda: nc.vector.tensor_tensor(out=out_ap, in0=self.t1[:], in1=self.t2[:], op=ALU.add), ["t1", "t2"], out_res)

    def proj_fm(self, wt, wres, col, nk, rhs_fn, rhs_res, n):
        b = self.rr("acc", 2)
        ps, psn = self.ps_acc[b], f"ps_acc{b}"
        for k in range(nk):
            self.mm(ps[:, 0:n], wt[:, k, col:col + 128], rhs_fn(k), k == 0, k == nk - 1, [wres] + rhs_res, [psn])
        return ps, psn

    def kv_phase(self, li, par):
        bnc = self.kv_b[par]
        hrhs = lambda k: self.hT[:, k, 0:512]
        for j in range(2):
            self.make_h(1 + 512 * j, 512, V_LN1)
            tok0 = 512 * j
            for hp in range(4):
                wt, rn = self.wnext()
                for hh in range(2):
                    h = 2 * hp + hh
                    ps, psn = self.proj_fm(wt, rn, hh * 128, KC, hrhs, ["hT"], 512)
                    sb_ = self.rr("kst", 2)
                    kst, ksn = self.kst[sb_], f"kst{sb_}"
                    self.head_norm(ps[:], psn, V_KNA, kst[:], [ksn], False)
                    self.dma("sp", self.ka_o[h][:, tok0:tok0 + 512], kst[:], [ksn], ["ka_o"])
                    if j == 0:
                        self.dma("sp", bnc[0:128, h * 256:h * 256 + 256], kst[:, 0:256], [ksn], ["kv_b"])
                    else:
                        self.dma("sp", bnc[128:256, h * 256:h * 256 + 256], kst[:, 256:512], [ksn], ["kv_b"])
            wt, rn = self.wnext()
            for g in range(2):
                ps, psn = self.proj_fm(wt, rn, g * 128, KC, hrhs, ["hT"], 512)
                self.head_norm(ps[:], psn, V_KNB, self.qn[:], ["qn"], False)
                sb_ = self.rr("kst", 2)
                kst, ksn = self.kst[sb_], f"kst{sb_}"
                self.rope(tok0, kst[:], [ksn])
                self.dma("sp", self.kb_o[g][:, tok0:tok0 + 512], kst[:], [ksn], ["kb_o"])
                c0 = 4096 + g * 128
                if j == 0:
                    self.dma("sp", bnc[0:128, c0:c0 + 128], kst[:, 0:128], [ksn], ["kv_b"])
                else:
                    self.dma("sp", bnc[128:256, c0:c0 + 128], kst[:, 384:512], [ksn], ["kv_b"])
            for vt in range(5):
                wt, rn = self.wnext()
                for tt in range(4):
                    b = self.rr("acc", 2)
                    ps, psn = self.ps_acc[b], f"ps_acc{b}"
                    for k in range(KC):
                        self.mm(ps[:, 0:256], self.hT[:, k, tt * 128:tt * 128 + 128], wt[:, k, :], k == 0, k == KC - 1,
                                [rn, "hT"], [psn])
                    sb_ = self.rr("vst", 2)
                    vst, vsn = self.vst[sb_], f"vst{sb_}"
                    self.act(vst[:], ps[:, 0:256], AF.Copy, [psn], [vsn])
                    tile_ = 4 * j + tt
                    if vt < 4:
                        self.dma("sp", self.va_o[tile_][:, vt * 256:vt * 256 + 256], vst[:], [vsn], ["va_o"])
                        if tile_ < 2:
                            cc = 2048 + tile_ * 1024 + vt * 256
                            self.dma("sp", bnc[0:128, cc:cc + 256], vst[:], [vsn], ["kv_b"])
                        if tile_ >= 6:
                            cc = 2048 + (tile_ - 6) * 1024 + vt * 256
                            self.dma("sp", bnc[128:256, cc:cc + 256], vst[:], [vsn], ["kv_b"])
                    else:
                        self.dma("sp", self.vb_o[tile_][:, :], vst[:], [vsn], ["vb_o"])
                        if tile_ == 0:
                            self.dma("sp", bnc[0:128, 4352:4608], vst[:], [vsn], ["kv_b"])
                        if tile_ == 7:
                            self.dma("sp", bnc[128:256, 4352:4608], vst[:], [vsn], ["kv_b"])
        self.allgather(bnc, self.kv_g[par], reads=["kv_b"], writes=["kv_g"])

    def load_halo(self, par, s):
        nc = self.nc
        self.tr.op("pool", lambda: nc.gpsimd.indirect_dma_start(
            out=self.halo, out_offset=None, in_=self.kv_g[par],
            in_offset=bass.IndirectOffsetOnAxis(ap=self.idx[:, s:s + 1], axis=0)),
            reads=["kv_g", "idx"], writes=["halo"], dma=True)

    def attn_half(self, li, j, par):
        nc = self.nc
        self.load_halo(par, j)
        self.make_h(1 + 512 * j, 512, V_LN1)
        hrhs = lambda k: self.hT[:, k, 0:512]
        ident, ones = self.cmb[:, 0:128], self.cmb[:, 128:256]
        halo = self.halo
        baseA = 0 if j == 0 else 2
        for hp in range(4):
            sl = self.rr("kA", 2)
            kA, vA, bia = self.kA[sl], self.vA[sl], self.bias[sl]
            self.dma("sp", kA, self.ka_o[2 * hp:2 * hp + 2, :, baseA * 128:baseA * 128 + 768].rearrange("h p t -> p h t"),
                     ["ka_o"], [f"kA{sl}"])
            self.dma("sp", vA, self.va_o[baseA:baseA + 6, :, hp * 256:hp * 256 + 256].rearrange("t p f -> p t f"),
                     ["va_o"], [f"vA{sl}"])
            self.dma("pool", bia, self.rpbT_d[li][:, 2 * hp:2 * hp + 2, 0:896], [], [f"bias{sl}"])
            wt, rn = self.wnext()
            for hh in range(2):
                h = 2 * hp + hh
                ps, psn = self.proj_fm(wt, rn, hh * 128, KC, hrhs, ["hT"], 512)
                qs = self.rr("qT", 2)
                qT = self.qT[qs][:, 0, :]
                self.head_norm(ps[:], psn, V_QNA, qT, [f"qT{qs}"], True)
                for bp in range(4):
                    b = 4 * j + bp
                    slots = _a_slots(b)
                    for si, dlt in enumerate(slots):
                        kt = b + dlt // 2
                        if kt < 0 or kt > 7:
                            ht = kt + 2 if kt < 0 else kt - 8
                            ktile = halo[:, h * 256 + ht * 128:h * 256 + ht * 128 + 128]
                            vtile = halo[:, 2048 + ht * 1024 + h * 128:2048 + ht * 1024 + h * 128 + 128]
                            kres = ["halo"]
                        else:
                            ktile = kA[:, hh, (kt - baseA) * 128:(kt - baseA) * 128 + 128]
                            vtile = vA[:, kt - baseA, hh * 128:hh * 128 + 128]
                            kres = [f"kA{sl}", f"vA{sl}"]
                        sreg = self.rr("ps_s", 8)
                        pss = self.ps_s[sreg // 4][:, (sreg % 4) * 128:(sreg % 4) * 128 + 128]
                        psn_s = f"ps_s{sreg}"
                        self.mm(pss, ktile, qT[:, bp * 128:bp * 128 + 128], True, False, kres + [f"qT{qs}"], [psn_s])
                        self.mm(pss, ident, bia[:, hh, (6 - dlt) * 64:(6 - dlt) * 64 + 128], False, False, ["cmb", f"bias{sl}"], [psn_s])
                        self.mm(pss, ident, self.negA[:, A_MIDX[(b, dlt)], :], False, True, ["cmb", "negA"], [psn_s])
                        pb = self.rr("pT", 3)
                        pT = self.pT[pb][:, 0:128]
                        self.act(pT, pss, AF.Exp, [psn_s], [f"pT{pb}"])
                        last = si == len(slots) - 1
                        self.mm(self.ps_o[:, bp * 128:bp * 128 + 128], vtile, pT, si == 0, last, kres + [f"pT{pb}"], ["ps_o"], sig=False)
                        self.mm(self.ps_d[:, bp * 128:bp * 128 + 128], ones, pT, si == 0, last, ["cmb", f"pT{pb}"], ["ps_d"],
                                sig=(last and bp == 3))
                self.dve(lambda: nc.vector.reciprocal(out=self.t3[:], in_=self.ps_d[:]), ["ps_d"], ["t3"])
                self.dve(lambda h=h: nc.vector.tensor_tensor(out=self.OT[:, h, :], in0=self.ps_o[:], in1=self.t3[:], op=ALU.mult),
                         ["ps_o", "ps_d", "t3"], ["OT"])
        baseB = 0 if j == 0 else 3
        sl = self.rr("kA", 2)
        kB = self.kA[sl][:, :, 0:640]
        vB = self.vA[sl][:, 0:5, :]
        self.dma("sp", kB, self.kb_o[:, :, baseB * 128:baseB * 128 + 640].rearrange("g p t -> p g t"), ["kb_o"], [f"kA{sl}"])
        self.dma("sp", vB, self.vb_o[baseB:baseB + 5, :, :].rearrange("t p f -> p t f"), ["vb_o"], [f"vA{sl}"])
        for g in range(2):
            qs = self.rr("qT", 2)
            qB = self.qT[qs]
            for wp in range(2):
                wt, rn = self.wnext()
                for hh in range(2):
                    ps, psn = self.proj_fm(wt, rn, hh * 128, KC, hrhs, ["hT"], 512)
                    self.head_norm(ps[:], psn, V_QNB, self.qn[:], ["qn"], True)
                    self.rope(512 * j, qB[:, 2 * wp + hh, :], [f"qT{qs}"])
            for bp in range(4):
                b = 4 * j + bp
                for jj in range(3):
                    kt = b - 1 + jj
                    if kt < 0 or kt > 7:
                        ktile = halo[:, 4096 + g * 128:4096 + g * 128 + 128]
                        vtile = halo[:, 4352 + g * 128:4352 + g * 128 + 128]
                        kres = ["halo"]
                    else:
                        ktile = kB[:, g, (kt - baseB) * 128:(kt - baseB) * 128 + 128]
                        vtile = vB[:, kt - baseB, g * 128:g * 128 + 128]
                        kres = [f"kA{sl}", f"vA{sl}"]
                    mi = None
                    if jj == 0:
                        mi = 0 if b == 0 else 1
                    elif jj == 2:
                        mi = 3 if b == 7 else 2
                    sreg = self.rr("ps_sB", 2)
                    pss = self.ps_s[sreg]
                    psn_s = [f"ps_s{4 * sreg + i}" for i in range(4)]
                    self.mm(pss[:].rearrange("p (h q) -> p h q", h=4), ktile, qB[:, :, bp * 128:bp * 128 + 128], True, mi is None,
                            kres + [f"qT{qs}"], psn_s, sig=(mi is None))
                    if mi is not None:
                        for hq in range(4):
                            self.mm(pss[:, hq * 128:hq * 128 + 128], ident, self.negB[:, mi, :], False, hq == 3, ["cmb", "negB"], psn_s,
                                    sig=(hq == 3))
                    pb = self.rr("pT", 3)
                    pT = self.pT[pb]
                    self.act(pT[:], pss[:], AF.Exp, psn_s, [f"pT{pb}"])
                    self.mm(self.ps_o[:], vtile, pT[:], jj == 0, jj == 2, kres + [f"pT{pb}"], ["ps_o"], sig=False)
                    self.mm(self.ps_d[:], ones, pT[:], jj == 0, jj == 2, ["cmb", f"pT{pb}"], ["ps_d"], sig=(jj == 2))
                for hq in range(4):
                    self.dve(lambda hq=hq, g=g: nc.vector.tensor_scalar(
                        out=self.t3[:, hq * 128:hq * 128 + 128], in0=self.ps_d[:, hq * 128:hq * 128 + 128],
                        scalar1=self.sinkexp[:, 4 * g + hq:4 * g + hq + 1], scalar2=None, op0=ALU.add), ["ps_d", "sinkexp"], ["t3"])
                self.dve(lambda: nc.vector.reciprocal(out=self.t2[:], in_=self.t3[:]), ["t3"], ["t2"])
                self.dve(lambda g=g, bp=bp: nc.vector.tensor_tensor(
                    out=self.OT[:, 8 + 4 * g:8 + 4 * g + 4, bp * 128:bp * 128 + 128],
                    in0=self.ps_o[:].rearrange("p (h q) -> p h q", h=4), in1=self.t2[:].rearrange("p (h q) -> p h q", h=4), op=ALU.mult),
                    ["ps_o", "ps_d", "t2"], ["OT"])
        if DEBUG and li == 0:
            self.dma("sp", self.dbg_ot[j], self.OT[:], ["OT"], ["dbg_ot"])
            self.dma("sp", self.dbg_halo[j], self.halo, ["halo"], ["dbg_halo"])
        for grp, goff in ((0, V_ONA), (1, V_ONB)):
            self.sumsq([self.OT[:, 8 * grp + c, :] for c in range(8)], 512, ["OT"])
            self.finish_rstd(512, 1.0 / 1024.0, EPS)
            for c in range(8):
                self.dve(lambda c=c, grp=grp, goff=goff: nc.vector.scalar_tensor_tensor(
                    out=self.OT[:, 8 * grp + c, :], in0=self.OT[:, 8 * grp + c, :], scalar=self.vcol(goff + c),
                    in1=self.rstd[:, 0:512], op0=ALU.mult, op1=ALU.mult), ["OT", "rstd", "vecs"], ["OT"])
        for np_ in range(8):
            wt, rn = self.wnext()
            for cc in range(2):
                ch = 2 * np_ + cc
                ps, psn = self.proj_fm(wt, rn, cc * 128, KC, lambda k: self.OT[:, k, :], ["OT"], 512)
                xs = self.x[:, ch, 1 + 512 * j:1 + 512 * j + 512]
                self.dve(lambda ps=ps, xs=xs: nc.vector.tensor_tensor(out=xs, in0=ps[:], in1=xs, op=ALU.add), [psn, "x"], ["x"])

    def xb_exchange(self, par):
        nc = self.nc
        self.dve(lambda: nc.vector.tensor_copy(out=self.xbs[:, 0, :], in_=self.x[:, :, 1]), ["x"], ["xbs"])
        self.dve(lambda: nc.vector.tensor_copy(out=self.xbs[:, 1, :], in_=self.x[:, :, T]), ["x"], ["xbs"])
        self.dma("sp", self.xb_b[par][0:128, :], self.xbs[:, 0, :], ["xbs"], ["xb_b"])
        self.dma("sp", self.xb_b[par][128:256, :], self.xbs[:, 1, :], ["xbs"], ["xb_b"])
        self.allgather(self.xb_b[par], self.xb_g[par], reads=["xb_b"], writes=["xb_g"])
        for s in range(2):
            self.tr.op("pool", lambda s=s: nc.gpsimd.indirect_dma_start(
                out=self.xh[:, s, :], out_offset=None, in_=self.xb_g[par],
                in_offset=bass.IndirectOffsetOnAxis(ap=self.idx[:, s:s + 1], axis=0)),
                reads=["xb_g", "idx"], writes=[f"xh{s}"], dma=True)
        self.dve(lambda: nc.vector.tensor_scalar(out=self.x[:, :, 0], in0=self.xh[:, 0, :], scalar1=self.flag[:, 0:1], scalar2=None, op0=ALU.mult),
                 ["xh0", "flag"], ["xhalo"])
        self.dve(lambda: nc.vector.tensor_scalar(out=self.x[:, :, T + 1], in0=self.xh[:, 1, :], scalar1=self.flag[:, 1:2], scalar2=None, op0=ALU.mult),
                 ["xh1", "flag"], ["xhalo"])

    def ffn_third(self, li, ti):
        nc = self.nc
        a, bnd = TH[ti]
        n = bnd - a
        nu = n + 2
        if ti > 0:
            self.dve(lambda: nc.vector.tensor_copy(out=self.xtmp[:], in_=self.x[:, :, a]), ["x"], ["xtmp"])
            self.dve(lambda: nc.vector.tensor_copy(out=self.x[:, :, a], in_=self.xsv[:, ti - 1, :]), ["xsv"], ["x"])
        self.make_h(a, nu, V_LN2)
        if ti > 0:
            self.dve(lambda: nc.vector.tensor_copy(out=self.x[:, :, a], in_=self.xtmp[:]), ["xtmp"], ["x"])
        for gi, (f0, nf) in enumerate(GROUPS):
            hb = self.rr("hg", 2)
            hg = self.hg[hb]
            for duo in range(nf // 2):
                f = f0 + 2 * duo
                wtg, rng_ = self.wnext()
                wtu, rnu = self.wnext()
                for cc in range(2):
                    fc = f + cc
                    cb = self.rr("cg", 2)
                    cg, cu, sg = self.cg[cb], self.cu[cb], self.sg[cb]
                    for (wt, rn, dst, dn, ch) in ((wtg, rng_, cg, f"cg{cb}", fc), (wtu, rnu, cu, f"cu{cb}", FC + fc)):
                        ps, psn = self.proj_fm(wt, rn, cc * 128, KC, lambda k: self.hT[:, k, 0:nu], ["hT"], nu)
                        w0 = self.vcol(V_CW + ch)
                        w1 = self.vcol(V_CW + 88 + ch)
                        w2 = self.vcol(V_CW + 176 + ch)
                        bb = self.vcol(V_CB + ch)
                        self.act(dst[:, 0:n], ps[:, 1:n + 1], AF.Identity, [psn, "vecs"], [dn], scale=w1, bias=bb)
                        self.dve(lambda ps=ps, dst=dst, w0=w0: nc.vector.scalar_tensor_tensor(
                            out=dst[:, 0:n], in0=ps[:, 0:n], scalar=w0, in1=dst[:, 0:n], op0=ALU.mult, op1=ALU.add),
                            [psn, "vecs", dn], [dn])
                        self.dve(lambda ps=ps, dst=dst, w2=w2: nc.vector.scalar_tensor_tensor(
                            out=dst[:, 0:n], in0=ps[:, 2:n + 2], scalar=w2, in1=dst[:, 0:n], op0=ALU.mult, op1=ALU.add),
                            [psn, "vecs", dn], [dn])
                    self.act(sg[:, 0:n], cg[:, 0:n], AF.Silu, [f"cg{cb}"], [f"sg{cb}"])
                    self.dve(lambda hg=hg, sg=sg, cu=cu, k=2 * duo + cc: nc.vector.tensor_tensor(
                        out=hg[:, k, 0:n], in0=sg[:, 0:n], in1=cu[:, 0:n], op=ALU.mult), [f"sg{cb}", f"cu{cb}"], [f"hg{hb}"])
            if DEBUG and li == 0 and ti == 0 and gi == 0:
                self.dma("sp", self.dbg_hg, hg, [f"hg{hb}"], ["dbg_hg"])
                self.dma("sp", self.dbg_h2, self.hT[:], ["hT"], ["dbg_h2"])
            for np_ in range(8):
                wt, rn = self.wnext()
                for cc in range(2):
                    ch = 2 * np_ + cc
                    ps, psn = self.proj_fm(wt, rn, cc * 128, nf, lambda k: hg[:, k, 0:n], [f"hg{hb}"], n)
                    xs = self.x[:, ch, 1 + a:1 + a + n]
                    self.dve(lambda ps=ps, xs=xs: nc.vector.tensor_tensor(out=xs, in0=ps[:, 0:n], in1=xs, op=ALU.add), [psn, "x"], ["x"])

    def build(self):
        L = self.L
        tr = self.tr
        for li in range(L):
            self.plan += self.plan_layer(li)
        self.gather_weight("in", 0, 0)
        self.prologue()
        for li in range(L):
            if li > 0:
                tr.new_epoch()
            par = li % 2
            self.dma("sp", self.vecs[:], self.vecs_d[li], [], ["vecs"])
            self.act(self.sinkexp[:], self.vcol(V_SINK, 8), AF.Exp, ["vecs"], ["sinkexp"])
            tr.alias(FFN_NAMES, ATT_NAMES)
            self.kv_phase(li, par)
            if li == 0:
                self.gather_weight("out", 0, 0)
            for j in range(2):
                self.attn_half(li, j, par)
            self.xb_exchange(par)
            nxt = ([0] if li == 0 else []) + ([li + 1] if li + 1 < L else [])
            for l2 in nxt:
                kinds = [("up", 0), ("up", 2), ("up", 1), ("up", 3), ("dn", 0), ("dn", 1)]
                if l2 != 0 or li != 0:
                    kinds = [("in", 0), ("out", 0)] + kinds
                for kind, q in kinds:
                    self.gather_weight(kind, l2, q)
            tr.alias(ATT_NAMES, FFN_NAMES)
            if DEBUG and li == 0:
                self.dma("sp", self.dbg_xmid, self.x[:], ["x", "xhalo"], ["dbg_xmid"])
            for k_, (a_, _) in enumerate(TH[1:]):
                self.dve(lambda k_=k_, a_=a_: self.nc.vector.tensor_copy(out=self.xsv[:, k_, :], in_=self.x[:, :, a_]), ["x"], ["xsv"])
            for ti in range(3):
                self.ffn_third(li, ti)
        assert self.plan_i == len(self.plan), (self.plan_i, len(self.plan))
        self.dma("sp", self.yT_d.rearrange("(k p) t -> p k t", p=128), self.x[:, :, 1:T + 1], ["x"], ["yT"])
        tr.wait_all("sp", ["yT"])
        for e in ("sp", "pool"):
            for sl in tr.rots[e]:
                if sl.cnt:
                    tr._wait(e, Tok(sl.sem, sl.cnt, "dma", sl.name))
        self.nc.all_engine_barrier()
        self.nc.clear_and_free_semaphores(tr.all_sems + [self.cc_sem])
        self.nc.all_engine_barrier()
        return self.nc


_PROG_CACHE = {}


def _get_prog(n_layers):
    if n_layers not in _PROG_CACHE:
        p = Prog(n_layers)
        p.build()
        _PROG_CACHE[n_layers] = p
    return _PROG_CACHE[n_layers]


def _tag(a, c):
    b = a.copy()
    b[..., 896] = float(c)
    return b


def _in_maps(xT_shards, inp, layers):
    vecs = _vecs(inp, layers)
    cmat = _cmat()
    rpbT = np.zeros((len(layers), 128, 8, 897), np.float32)
    for i_, l in enumerate(layers):
        rpbT[i_, :, :, 0:896] = _rpb_table(inp["rpb"][l])
    maps = []
    p = np.arange(128)
    for c in range(NC):
        idx = np.stack([max(c - 1, 0) * 256 + 128 + p, min(c + 1, NC - 1) * 256 + p], axis=1).astype(np.int32)
        flag = np.ones((128, 2), np.float32)
        if c == 0:
            flag[:, 0] = 0.0
        if c == NC - 1:
            flag[:, 1] = 0.0
        maps.append({
            "xT": xT_shards[c],
            "pos": np.ascontiguousarray(inp["positions"][c * T:(c + 1) * T]).astype(np.int32),
            "vecs": vecs, "cmat": cmat, "negA": _neg_a(c), "negB": _neg_b(c), "rpbT": _tag(rpbT, c), "idx": idx, "flag": flag,
            "w_in": np.ascontiguousarray(inp["w_in"][layers][:, c * 256:(c + 1) * 256, :]),
            "w_out": np.ascontiguousarray(inp["w_out"][layers][:, c * 256:(c + 1) * 256, :]),
            "w_up": np.ascontiguousarray(inp["w_up"][layers][:, c * 256:(c + 1) * 256, :]),
            "w_down": np.ascontiguousarray(inp["w_down"][layers][:, c * 704:(c + 1) * 704, :]),
        })
    return maps


LAYERS_PER_LAUNCH = 4
DEBUG = False


def kernel(**inputs):
    inp = {k: np.asarray(v) for k, v in inputs.items()}
    x = inp["x"]
    depth = inp["w_in"].shape[0]
    xT = [np.ascontiguousarray(x[0, c * T:(c + 1) * T, :].T) for c in range(NC)]
    l0 = 0
    while l0 < depth:
        layers = list(range(l0, min(depth, l0 + LAYERS_PER_LAUNCH)))
        prog = _get_prog(len(layers))
        res = run_bass_kernel_spmd(prog.nc, _in_maps(xT, inp, layers), core_ids=list(range(NC)))
        xT = [np.ascontiguousarray(res.results[c]["yT"]) for c in range(NC)]
        l0 += len(layers)
    out = np.concatenate([xT[c].T for c in range(NC)], axis=0)[None]
    return np.ascontiguousarray(out).astype(np.float32)
```

```python
import math
import numpy as np
from contextlib import ExitStack

import concourse.bass as bass
import concourse.mybir as mybir
from concourse.bass_utils import run_bass_kernel_spmd

F32 = mybir.dt.float32
BF16 = mybir.dt.bfloat16
I32 = mybir.dt.int32
AF = mybir.ActivationFunctionType
ALU = mybir.AluOpType

NC = 8
T = 1024
D = 2048
KC = 16
DFF = 5632
FC = 44
PW = 4608
EPS = 1e-6
NVL = 412
NEG = -30000.0
TH = [(0, 342), (342, 684), (684, 1024)]
GROUPS = [(0, 12), (12, 12), (24, 12), (36, 8)]


class Tok:
    __slots__ = ("sem", "val", "eng", "key")

    def __init__(self, sem, val, eng, key):
        self.sem, self.val, self.eng, self.key = sem, val, eng, key


class Slot:
    def __init__(self, sem, name):
        self.sem, self.cnt, self.name = sem, 0, name


class Tracker:
    def __init__(self, nc):
        self.nc = nc
        self.eng = {"pe": nc.tensor, "act": nc.scalar, "dve": nc.vector, "pool": nc.gpsimd, "sp": nc.sync}
        self.sem, self.cnt, self.semkey = {}, {}, {}
        self.seen = {e: {} for e in self.eng}
        self.res = {}
        self.pending = {e: [] for e in self.eng}
        self.epoch = 0
        self.rots = {}
        self.roti = {}
        self.all_sems = []
        for e, n in (("sp", 12), ("pool", 6), ("act", 4)):
            self.rots[e] = [Slot(nc.alloc_semaphore(f"rot_{e}_{i}"), f"rot_{e}_{i}") for i in range(n)]
            self.all_sems += [sl.sem for sl in self.rots[e]]
            self.roti[e] = 0
        self.new_epoch()

    def new_epoch(self):
        for e in self.eng:
            assert not self.pending[e], e
            self.semkey[e] = f"s_{e}_{self.epoch}"
            self.sem[e] = self.nc.alloc_semaphore(self.semkey[e])
            self.all_sems.append(self.sem[e])
            self.cnt[e] = 0
        self.epoch += 1

    def _wait(self, e, tok):
        if tok is None:
            return
        if tok.eng == e and e == "pe":
            return
        assert tok.val is not None, "dependency on an unsignalled instruction"
        if self.seen[e].get(tok.key, 0) >= tok.val:
            return
        self.eng[e].wait_ge(tok.sem, tok.val)
        self.seen[e][tok.key] = tok.val

    def rot(self, e):
        sl = self.rots[e][self.roti[e] % len(self.rots[e])]
        self.roti[e] += 1
        if sl.cnt:
            self._wait(e, Tok(sl.sem, sl.cnt, "dma", sl.name))
        return sl

    def op(self, e, fn, reads=(), writes=(), sig=True, dma=False):
        for r in reads:
            st = self.res.get(r)
            if st:
                self._wait(e, st[0])
        for w in writes:
            st = self.res.get(w)
            if st:
                self._wait(e, st[0])
                for t in st[1].values():
                    self._wait(e, t)
        slot = self.rot(e) if dma else None
        ins = fn()
        if dma:
            slot.cnt += 16
            ins.then_inc(slot.sem, 16)
            tok = Tok(slot.sem, slot.cnt, "dma", slot.name)
        elif sig:
            self.cnt[e] += 1
            ins.then_inc(self.sem[e], 1)
            tok = Tok(self.sem[e], self.cnt[e], e, self.semkey[e])
            for p in self.pending[e]:
                p.sem, p.val, p.key = tok.sem, tok.val, tok.key
            self.pending[e] = []
        else:
            tok = Tok(None, None, e, None)
            self.pending[e].append(tok)
        for r in reads:
            st = self.res.setdefault(r, [None, {}])
            st[1][(tok.eng, id(tok)) if tok.key is None else tok.key] = tok
        for w in writes:
            self.res[w] = [tok, {}]
        return tok

    def alias(self, src, dst):
        toks = {}
        for n in list(src) + list(dst):
            st = self.res.get(n)
            if not st:
                continue
            for t in [st[0]] + list(st[1].values()):
                if t is None:
                    continue
                assert t.val is not None
                if t.key not in toks or toks[t.key].val < t.val:
                    toks[t.key] = t
        for n in dst:
            self.res[n] = [None, dict(toks)]

    def wait_all(self, e, names):
        for n in names:
            st = self.res.get(n)
            if st:
                self._wait(e, st[0])
                for t in st[1].values():
                    self._wait(e, t)


def _a_slots(b):
    s = [-4, -2, 0, 2, 4]
    if b == 0:
        s = s + [6]
    if b == 7:
        s = [-6] + s
    return s


def _a_mask_index():
    idx = {}
    n = 0
    for d in (-4, -2, 0, 2, 4):
        for b in (2, 3, 4, 5):
            idx[(b, d)] = n
        n += 1
    for b in (0, 1, 6, 7):
        for d in _a_slots(b):
            idx[(b, d)] = n
            n += 1
    return idx, n


A_MIDX, N_AMASK = _a_mask_index()


def _neg_a(core):
    out = np.zeros((128, N_AMASK, 128), np.float32)
    kk = np.arange(128)
    ki, kc = kk // 64, kk % 64
    qi, qc = kk // 64, kk % 64
    done = set()
    for (b, d), m in A_MIDX.items():
        if m in done:
            continue
        done.add(m)
        r0 = 16 * core + 2 * b
        kr = (r0 + d + ki)[:, None]
        qr = (r0 + qi)[None, :]
        rs = np.clip(qr - 4, 0, 120)
        cs = np.clip(qc - 8, 0, 48)[None, :]
        valid = (kr >= 0) & (kr < 128) & (kr >= rs) & (kr < rs + 8) & (kc[:, None] >= cs) & (kc[:, None] < cs + 16)
        out[:, m, :] = np.where(valid, 0.0, NEG)
    return out


def _neg_b(core):
    out = np.zeros((128, 4, 128), np.float32)
    kk = np.arange(128)[:, None]
    qq = np.arange(128)[None, :]
    g0 = np.where(kk >= qq, 0.0, NEG)
    g2 = np.where(kk <= qq, 0.0, NEG)
    out[:, 0, :] = NEG if core == 0 else g0
    out[:, 1, :] = g0
    out[:, 2, :] = g2
    out[:, 3, :] = NEG if core == NC - 1 else g2
    return out


def _rpb_table(rpb_l):
    p = np.arange(128)
    i, kc = p // 64, p % 64
    m = np.arange(14)
    qc = np.arange(64)
    dr = 6 - m[None, :, None] + i[:, None, None]
    dc = kc[:, None, None] - qc[None, None, :]
    ok = (np.abs(dr) <= 7) & (np.abs(dc) <= 15)
    dri = np.clip(dr + 7, 0, 14) + 0 * dc
    dci = np.clip(dc + 15, 0, 30) + 0 * dr
    g = rpb_l[:, dri, dci]
    g = np.where(ok[None], g, np.float32(0.0))
    return np.ascontiguousarray(g.transpose(1, 0, 2, 3).reshape(128, 8, 896)).astype(np.float32)


def _cmat():
    c = np.zeros((128, 385), np.float32)
    c[:, 0:128] = np.eye(128, dtype=np.float32)
    c[:, 128:256] = 1.0
    for m_ in range(128):
        if m_ < 64:
            c[m_ + 64, 256 + m_] = -1.0
        else:
            c[m_ - 64, 256 + m_] = 1.0
    j = np.arange(128) % 64
    c[:, 384] = (10000.0 ** (-(2.0 * j) / 128.0)).astype(np.float32)
    return c


V_LN1, V_LN2, V_ONA, V_ONB, V_QNA, V_KNA, V_QNB, V_KNB, V_CW, V_CB, V_SINK = 0, 16, 32, 40, 48, 49, 50, 51, 52, 316, 404


def _vecs(inp, layers):
    out = []
    for l in layers:
        cols = [inp["ln1_g"][l].reshape(16, 128).T, inp["ln2_g"][l].reshape(16, 128).T,
                inp["on_a"][l].reshape(8, 128).T, inp["on_b"][l].reshape(8, 128).T]
        for k in ("qn_a", "kn_a", "qn_b", "kn_b"):
            cols.append(inp[k][l].reshape(128, 1))
        cols.append(inp["conv_w"][l].reshape(3, 88, 128).transpose(2, 0, 1).reshape(128, 264))
        cols.append(inp["conv_b"][l].reshape(88, 128).T)
        cols.append(np.broadcast_to(inp["sink"][l][None, :], (128, 8)))
        out.append(np.concatenate([np.asarray(c, np.float32) for c in cols], axis=1))
    return np.ascontiguousarray(np.stack(out, axis=0))


U_KA, U_VA, U_BIAS, U_QT, U_HALO, U_END = 0, 3072, 6144, 9728, 13824, 18432
U_HG, U_CG = 0, 8256
ATT_NAMES = ["kA0", "kA1", "vA0", "vA1", "bias0", "bias1", "qT0", "qT1", "halo"]
FFN_NAMES = ["hg0", "hg1", "cg0", "cg1", "cu0", "cu1", "sg0", "sg1"]


class Prog:
    def __init__(self, n_layers):
        self.L = n_layers
        nc = self.nc = bass.Bass("TRN2", target_bir_lowering=False)
        L = n_layers
        dt = nc.dram_tensor
        self.xT_d = dt("xT", [D, T], F32, kind="ExternalInput").ap()
        self.pos_d = dt("pos", [T], I32, kind="ExternalInput").ap()
        self.vecs_d = dt("vecs", [L, 128, NVL], F32, kind="ExternalInput").ap()
        self.cmat_d = dt("cmat", [128, 385], F32, kind="ExternalInput").ap()
        self.negA_d = dt("negA", [128, N_AMASK, 128], F32, kind="ExternalInput").ap()
        self.negB_d = dt("negB", [128, 4, 128], F32, kind="ExternalInput").ap()
        self.rpbT_d = dt("rpbT", [L, 128, 8, 897], F32, kind="ExternalInput").ap()
        self.idx_d = dt("idx", [128, 2], I32, kind="ExternalInput").ap()
        self.flag_d = dt("flag", [128, 2], F32, kind="ExternalInput").ap()
        self.win_d = dt("w_in", [L, D, 768], F32, kind="ExternalInput").ap()
        self.wout_d = dt("w_out", [L, D, 256], F32, kind="ExternalInput").ap()
        self.wup_d = dt("w_up", [L, D, 1536], F32, kind="ExternalInput").ap()
        self.wdn_d = dt("w_down", [L, DFF, 256], F32, kind="ExternalInput").ap()
        self.yT_d = dt("yT", [D, T], F32, kind="ExternalOutput").ap()
        self.wb, self.wg = {}, {}
        shapes = {("in", 0): (3, 4096), ("out", 0): (1, 4096), ("up", 0): (2, 4096), ("up", 1): (2, 4096), ("up", 2): (2, 4096),
                  ("dn", 0): (2, 3072), ("dn", 1): (1, 3072), ("dn", 2): (1, 2048)}
        for l in range(L):
            for (kind, q), (ns, w) in shapes.items():
                self.wb[(kind, l, q)] = dt(f"wb_{kind}{l}_{q}", [ns * 128, w], BF16).ap()
                self.wg[(kind, l, q)] = dt(f"wg_{kind}{l}_{q}", [NC * ns * 128, w], BF16).ap()
        self.kv_b = [dt(f"kv_b{i}", [256, PW], BF16).ap() for i in range(2)]
        self.kv_g = [dt(f"kv_g{i}", [2048, PW], BF16).ap() for i in range(2)]
        self.xb_b = [dt(f"xb_b{i}", [256, 16], F32).ap() for i in range(2)]
        self.xb_g = [dt(f"xb_g{i}", [2048, 16], F32).ap() for i in range(2)]
        dk = dict(kind="ExternalOutput") if DEBUG else {}
        self.ka_o = dt("ka_own", [8, 128, T], BF16, **dk).ap()
        self.va_o = dt("va_own", [8, 128, 1024], BF16, **dk).ap()
        self.kb_o = dt("kb_own", [2, 128, T], BF16, **dk).ap()
        self.vb_o = dt("vb_own", [8, 128, 256], BF16, **dk).ap()
        if DEBUG:
            self.dbg_ot = dt("dbg_ot", [2, 128, 16, 512], BF16, kind="ExternalOutput").ap()
            self.dbg_halo = dt("dbg_halo", [2, 128, PW], BF16, kind="ExternalOutput").ap()
            self.dbg_xmid = dt("dbg_xmid", [128, KC, T + 2], F32, kind="ExternalOutput").ap()
            self.dbg_hg = dt("dbg_hg", [128, 12, 344], BF16, kind="ExternalOutput").ap()
            self.dbg_h2 = dt("dbg_h2", [128, KC, 512], BF16, kind="ExternalOutput").ap()

        sb = nc.alloc_sbuf_tensor
        self.x = sb("x", [128, KC, T + 2], F32)
        self.vecs = sb("vecs_sb", [128, NVL], F32)
        self.cm32 = sb("cm32", [128, 385], F32)
        self.cmb = sb("cmb", [128, 256], BF16)
        self.negA = sb("negA_sb", [128, N_AMASK, 128], BF16)
        self.negB = sb("negB_sb", [128, 4, 128], BF16)
        self.idx = sb("idx_sb", [128, 2], I32)
        self.flag = sb("flag_sb", [128, 2], F32)
        self.cos = sb("cos_sb", [128, T], F32)
        self.sin = sb("sin_sb", [128, T], F32)
        self.sinkexp = sb("sinkexp", [128, 8], F32)
        self.hT = sb("hT", [128, KC, 512], BF16)
        self.OT = sb("OT", [128, 16, 512], BF16)
        self.U = sb("U", [128, U_END], BF16)
        self.ring = [sb(f"ring{i}", [128, KC, 256], BF16) for i in range(4)]
        self.sq = [sb(f"sq{i}", [128, 512], BF16) for i in range(2)]
        self.rt = sb("rt", [128, 512], F32)
        self.rstd = sb("rstd", [128, 512], F32)
        self.qn = sb("qn", [128, 512], F32)
        self.t1 = sb("t1", [128, 512], F32)
        self.t2 = sb("t2", [128, 512], F32)
        self.t3 = sb("t3", [128, 512], F32)
        self.kst = [sb(f"kst{i}", [128, 512], BF16) for i in range(2)]
        self.vst = [sb(f"vst{i}", [128, 256], BF16) for i in range(2)]
        self.pT = [sb(f"pT{i}", [128, 512], BF16) for i in range(4)]
        self.xbs = sb("xbs", [128, 2, 16], F32)
        self.xh = sb("xh", [128, 2, 16], F32)
        self.scr = sb("scr", [128, 8], F32)
        self.xsv = sb("xsv", [128, 2, 16], F32)
        self.xtmp = sb("xtmp", [128, 16], F32)
        U = self.U
        v3 = lambda a, n0, n1: U[:, a:a + n0 * n1].rearrange("p (h t) -> p h t", h=n0)
        self.kA = [v3(U_KA + i * 1536, 2, 768) for i in range(2)]
        self.vA = [v3(U_VA + i * 1536, 6, 256) for i in range(2)]
        self.bias = [v3(U_BIAS + i * 1792, 2, 896) for i in range(2)]
        self.qT = [v3(U_QT + i * 2048, 4, 512) for i in range(2)]
        self.halo = U[:, U_HALO:U_HALO + PW]
        self.hg = [v3(U_HG + i * 4128, 12, 344) for i in range(2)]
        f32v = lambda a: U[:, a:a + 688].bitcast(F32)
        self.cg = [f32v(U_CG + i * 688) for i in range(2)]
        self.cu = [f32v(U_CG + (2 + i) * 688) for i in range(2)]
        self.sg = [f32v(U_CG + (4 + i) * 688) for i in range(2)]
        ps = nc.alloc_psum_tensor
        self.ps_acc = [ps(f"ps_acc{i}", [128, 512], F32) for i in range(2)]
        self.ps_ss = ps("ps_ss", [128, 512], F32)
        self.ps_s = [ps(f"ps_s{i}", [128, 512], F32) for i in range(2)]
        self.ps_o = ps("ps_o", [128, 512], F32)
        self.ps_d = ps("ps_d", [128, 512], F32)
        self.ps_r = ps("ps_r", [128, 512], F32)
        self.tr = Tracker(nc)
        self.cc_sem = nc.alloc_semaphore("cc_sem")
        self.cc_cnt = 0
        self.cnt = {}
        self.plan = []
        self.plan_i = 0
        self.load_i = 0
        self.loaded = {}

    def rr(self, key, n):
        v = self.cnt.get(key, 0)
        self.cnt[key] = v + 1
        return v % n

    def vcol(self, off, n=1):
        return self.vecs[:, off:off + n]

    def mm(self, out, lhsT, rhs, start, stop, reads, writes, sig=None):
        sig = stop if sig is None else sig
        return self.tr.op("pe", lambda: self.nc.tensor.matmul(out, lhsT=lhsT, rhs=rhs, start=start, stop=stop),
                          reads=reads, writes=writes, sig=sig)

    def act(self, out, in_, func, reads, writes, **kw):
        return self.tr.op("act", lambda: self.nc.scalar.activation(out=out, in_=in_, func=func, **kw), reads=reads, writes=writes)

    def dve(self, fn, reads, writes):
        return self.tr.op("dve", fn, reads=reads, writes=writes)

    def dma(self, e, out, in_, reads, writes, **kw):
        return self.tr.op(e, lambda: self.tr.eng[e].dma_start(out=out, in_=in_, **kw), reads=reads, writes=writes, dma=True)

    def pool_signal(self, reads, writes):
        return self.tr.op("pool", lambda: self.nc.gpsimd.memset(self.scr[:, 0:1], 0.0), reads=list(reads),
                          writes=list(writes) + ["scr"])

    def allgather(self, src, dst, reads, writes):
        self.tr.wait_all("pool", list(reads) + list(writes))
        self.nc.gpsimd.collective_compute("AllGather", ALU.bypass, replica_groups=[list(range(NC))],
                                          ins=[src.opt()], outs=[dst.opt()]).then_inc(self.cc_sem)
        self.cc_cnt += 1
        self.nc.gpsimd.wait_ge(self.cc_sem, self.cc_cnt)
        self.pool_signal(reads, writes)

    def _emit_loads(self, upto):
        while self.load_i < min(upto, len(self.plan)):
            src, nk, res = self.plan[self.load_i]
            i = self.load_i % 4
            if res not in self.tr.res:
                break
            self.dma("sp", self.ring[i][:, 0:nk, :], src[:, 0:nk * 256].rearrange("p (k n) -> p k n", n=256), [res], [f"ring{i}"])
            self.load_i += 1

    def wnext(self):
        self._emit_loads(self.plan_i + 3)
        assert self.load_i > self.plan_i, "weight tile consumed before its gather was emitted"
        i = self.plan_i % 4
        self.plan_i += 1
        return self.ring[i], f"ring{i}"

    def plan_layer(self, li):
        P = []
        def t_in(t):
            return (self.wg[("in", li, 0)][t * 128:(t + 1) * 128, :], KC, f"wg_in{li}_0")
        def t_out(t):
            return (self.wg[("out", li, 0)][t * 128:(t + 1) * 128, :], KC, f"wg_out{li}_0")
        def t_up(t):
            q, rb = (t % 6) // 2, (t // 6) * 2 + t % 2
            return (self.wg[("up", li, q)][rb * 128:(rb + 1) * 128, :], KC, f"wg_up{li}_{q}")
        def t_dn(gi, np_):
            if gi < 2:
                return (self.wg[("dn", li, 0)][(np_ * 2 + gi) * 128:(np_ * 2 + gi + 1) * 128, :], 12, f"wg_dn{li}_0")
            if gi == 2:
                return (self.wg[("dn", li, 1)][np_ * 128:(np_ + 1) * 128, :], 12, f"wg_dn{li}_1")
            return (self.wg[("dn", li, 2)][np_ * 128:(np_ + 1) * 128, :], 8, f"wg_dn{li}_2")
        for j in range(2):
            for hp in range(4):
                P.append(t_in(4 + hp))
            P.append(t_in(16))
            for vt in range(5):
                P.append(t_in(8 + vt if vt < 4 else 17))
        for j in range(2):
            for hp in range(4):
                P.append(t_in(hp))
            for g in range(2):
                for wp in range(2):
                    P.append(t_in(12 + 2 * g + wp))
            for np_ in range(8):
                P.append(t_out(np_))
        for ti in range(3):
            for gi, (f0, nf) in enumerate(GROUPS):
                for duo in range(nf // 2):
                    f = f0 + 2 * duo
                    P.append(t_up(f // 2))
                    P.append(t_up(22 + f // 2))
                for np_ in range(8):
                    P.append(t_dn(gi, np_))
        return P

    def gather_weight(self, kind, li, q):
        wb, wg = self.wb[(kind, li, q)], self.wg[(kind, li, q)]
        rb, rg = f"wb_{kind}{li}_{q}", f"wg_{kind}{li}_{q}"
        if kind == "dn":
            r0, ng, nk = {0: (0, 2, 12), 1: (24 * 128, 1, 12), 2: (36 * 128, 1, 8)}[q]
            for g in range(ng):
                src = self.wdn_d[li][r0 + g * nk * 128:r0 + (g + 1) * nk * 128, :].rearrange("(k p) c -> p k c", p=128)
                dst = wb[g * 128:(g + 1) * 128, :].rearrange("p (k c) -> p k c", c=256)
                self.dma("pool", dst, src, [], [rb])
        else:
            srcw, ns, c0 = {"in": (self.win_d, 3, 0), "out": (self.wout_d, 1, 0), "up": (self.wup_d, 2, q * 512)}[kind]
            for sl in range(ns):
                src = srcw[li][:, c0 + sl * 256:c0 + (sl + 1) * 256].rearrange("(k p) c -> p k c", p=128)
                dst = wb[sl * 128:(sl + 1) * 128, :].rearrange("p (k c) -> p k c", c=256)
                self.dma("pool", dst, src, [], [rb])
        self.allgather(wb, wg, reads=[rb], writes=[rg])

    def prologue(self):
        nc = self.nc
        pi = math.pi
        self.dma("sp", self.x[:, :, 1:T + 1], self.xT_d.rearrange("(k p) t -> p k t", p=128), [], ["x"])
        self.dma("sp", self.cm32[:], self.cmat_d, [], ["cm32"])
        self.dma("sp", self.idx[:], self.idx_d, [], ["idx"])
        self.dma("sp", self.flag[:], self.flag_d, [], ["flag"])
        self.dma("pool", self.negA[:, 0:16, :], self.negA_d[:, 0:16, :], [], ["negA"])
        self.dma("pool", self.negA[:, 16:N_AMASK, :], self.negA_d[:, 16:N_AMASK, :], [], ["negA"])
        self.dma("pool", self.negB[:], self.negB_d, [], ["negB"])
        self.dve(lambda: nc.vector.tensor_copy(out=self.cmb[:], in_=self.cm32[:, 0:256]), ["cm32"], ["cmb"])
        self.dve(lambda: nc.vector.memset(self.x[:, :, 0:T + 2:T + 1], 0.0), [], ["xhalo"])
        invf = self.cm32[:, 384:385]
        posi = self.t1[:].bitcast(I32)
        ang, kf, r, pf = self.qn, self.t3, self.rt, self.rstd
        for hf in range(2):
            cs = slice(hf * 512, hf * 512 + 512)
            self.dma("sp", posi, self.pos_d[hf * 512:hf * 512 + 512].partition_broadcast(128), [], ["t1"])
            self.dve(lambda: nc.vector.tensor_copy(out=pf[:], in_=posi), ["t1"], ["rstd"])
            self.dve(lambda: nc.vector.tensor_scalar(out=ang[:], in0=pf[:], scalar1=invf, scalar2=None, op0=ALU.mult),
                     ["rstd", "cm32"], ["qn"])
            self.dve(lambda: nc.vector.tensor_scalar(out=posi, in0=ang[:], scalar1=1.0 / (2 * pi), scalar2=None, op0=ALU.mult),
                     ["qn"], ["t1"])
            self.dve(lambda: nc.vector.tensor_copy(out=kf[:], in_=posi), ["t1"], ["t3"])
            c1 = float(np.float32(2 * pi))
            c2 = float(2 * pi - c1)
            self.dve(lambda: nc.vector.scalar_tensor_tensor(out=r[:], in0=kf[:], scalar=-c1, in1=ang[:], op0=ALU.mult, op1=ALU.add),
                     ["t3", "qn"], ["rt"])
            self.dve(lambda: nc.vector.scalar_tensor_tensor(out=r[:], in0=kf[:], scalar=-c2, in1=r[:], op0=ALU.mult, op1=ALU.add),
                     ["t3", "rt"], ["rt"])
            self.dve(lambda: nc.vector.tensor_scalar(out=kf[:], in0=r[:], scalar1=pi, scalar2=-2 * pi, op0=ALU.is_gt, op1=ALU.mult), ["rt"], ["t3"])
            self.dve(lambda: nc.vector.tensor_tensor(out=r[:], in0=r[:], in1=kf[:], op=ALU.add), ["rt", "t3"], ["rt"])
            self.dve(lambda: nc.vector.tensor_scalar(out=kf[:], in0=r[:], scalar1=-pi, scalar2=2 * pi, op0=ALU.is_lt, op1=ALU.mult), ["rt"], ["t3"])
            self.dve(lambda: nc.vector.tensor_tensor(out=r[:], in0=r[:], in1=kf[:], op=ALU.add), ["rt", "t3"], ["rt"])
            self.act(self.sin[:, cs], r[:], AF.Sin, ["rt"], ["sin"])
            self.dve(lambda: nc.vector.tensor_scalar(out=ang[:], in0=r[:], scalar1=pi / 2, scalar2=None, op0=ALU.add), ["rt"], ["qn"])
            self.dve(lambda: nc.vector.tensor_scalar(out=kf[:], in0=ang[:], scalar1=pi, scalar2=-2 * pi, op0=ALU.is_gt, op1=ALU.mult), ["qn"], ["t3"])
            self.dve(lambda: nc.vector.tensor_tensor(out=ang[:], in0=ang[:], in1=kf[:], op=ALU.add), ["qn", "t3"], ["qn"])
            self.act(self.cos[:, cs], ang[:], AF.Sin, ["qn"], ["cos"])

    def sumsq(self, chunks, n, res_in):
        nch = len(chunks)
        for i, c in enumerate(chunks):
            b = self.rr("sq", 2)
            self.act(self.sq[b][:, 0:n], c, AF.Square, res_in, [f"sq{b}"])
            self.mm(self.ps_ss[:, 0:n], self.cmb[:, 128:256], self.sq[b][:, 0:n], i == 0, i == nch - 1, [f"sq{b}", "cmb"], ["ps_ss"], sig=True)

    def finish_rstd(self, n, scale, bias):
        nc = self.nc
        self.act(self.rt[:, 0:n], self.ps_ss[:, 0:n], AF.Sqrt, ["ps_ss"], ["rt"], scale=scale, bias=bias)
        self.dve(lambda: nc.vector.reciprocal(out=self.rstd[:, 0:n], in_=self.rt[:, 0:n]), ["rt"], ["rstd"])

    def make_h(self, c0, n, goff):
        nc = self.nc
        self.sumsq([self.x[:, kc, c0:c0 + n] for kc in range(KC)], n, ["x", "xhalo"])
        self.finish_rstd(n, 1.0 / D, EPS)
        for kc in range(KC):
            self.dve(lambda kc=kc: nc.vector.scalar_tensor_tensor(
                out=self.hT[:, kc, 0:n], in0=self.x[:, kc, c0:c0 + n], scalar=self.vcol(goff + kc),
                in1=self.rstd[:, 0:n], op0=ALU.mult, op1=ALU.mult), ["x", "xhalo", "rstd", "vecs"], ["hT"])

    def head_norm(self, ps, psn, gcol, out_ap, out_res, fold_scale, n=512):
        nc = self.nc
        b = self.rr("sq", 2)
        self.act(self.sq[b][:, 0:n], ps, AF.Square, psn, [f"sq{b}"])
        self.mm(self.ps_ss[:, 0:n], self.cmb[:, 128:256], self.sq[b][:, 0:n], True, True, [f"sq{b}", "cmb"], ["ps_ss"])
        if fold_scale:
            self.finish_rstd(n, 1.0, 128.0 * EPS)
        else:
            self.finish_rstd(n, 1.0 / 128.0, EPS)
        self.dve(lambda: nc.vector.scalar_tensor_tensor(out=out_ap, in0=ps, scalar=self.vcol(gcol), in1=self.rstd[:, 0:n],
                                                        op0=ALU.mult, op1=ALU.mult), psn + ["rstd", "vecs"], out_res)

    def rope(self, tok0, out_ap, out_res):
        nc = self.nc
        cs = slice(tok0, tok0 + 512)
        self.mm(self.ps_r[:], self.cm32[:, 256:384], self.qn[:], True, True, ["qn", "cm32"], ["ps_r"])
        self.dve(lambda: nc.vector.tensor_tensor(out=self.t1[:], in0=self.qn[:], in1=self.cos[:, cs], op=ALU.mult), ["qn", "cos"], ["t1"])
        self.dve(lambda: nc.vector.tensor_tensor(out=self.t2[:], in0=self.ps_r[:], in1=self.sin[:, cs], op=ALU.mult), ["ps_r", "sin"], ["t2"])
        self.dve(lambda: nc.vector.tensor_tensor(out=out_ap, in0=self.t1[:], in1=self.t2[:], op=ALU.add), ["t1", "t2"], out_res)

    def acc_bufs(self, wide):
        bufs = [(self.ps_acc[0], ["ps_acc0"]), (self.ps_acc[1], ["ps_acc1"])]
        if wide:
            bufs += [(self.ps_s[0], [f"ps_s{i}" for i in range(4)]), (self.ps_s[1], [f"ps_s{4 + i}" for i in range(4)]),
                     (self.ps_o, ["ps_o"]), (self.ps_d, ["ps_d"])]
        return bufs

    def proj_fm(self, wt, wres, col, nk, rhs_fn, rhs_res, n, wide=False):
        bufs = self.acc_bufs(wide)
        b = self.rr("accw" if wide else "acc", len(bufs))
        ps, psn = bufs[b]
        for k in range(nk):
            self.mm(ps[:, 0:n], wt[:, k, col:col + 128], rhs_fn(k), k == 0, k == nk - 1, [wres] + rhs_res, psn)
        return ps, psn

    def kv_phase(self, li, par):
        bnc = self.kv_b[par]
        hrhs = lambda k: self.hT[:, k, 0:512]
        for j in range(2):
            self.make_h(1 + 512 * j, 512, V_LN1)
            tok0 = 512 * j
            for hp in range(4):
                wt, rn = self.wnext()
                for hh in range(2):
                    h = 2 * hp + hh
                    ps, psn = self.proj_fm(wt, rn, hh * 128, KC, hrhs, ["hT"], 512)
                    sb_ = self.rr("kst", 2)
                    kst, ksn = self.kst[sb_], f"kst{sb_}"
                    self.head_norm(ps[:], psn, V_KNA, kst[:], [ksn], False)
                    self.dma("sp", self.ka_o[h][:, tok0:tok0 + 512], kst[:], [ksn], ["ka_o"])
                    if j == 0:
                        self.dma("sp", bnc[0:128, h * 256:h * 256 + 256], kst[:, 0:256], [ksn], ["kv_b"])
                    else:
                        self.dma("sp", bnc[128:256, h * 256:h * 256 + 256], kst[:, 256:512], [ksn], ["kv_b"])
            wt, rn = self.wnext()
            for g in range(2):
                ps, psn = self.proj_fm(wt, rn, g * 128, KC, hrhs, ["hT"], 512)
                self.head_norm(ps[:], psn, V_KNB, self.qn[:], ["qn"], False)
                sb_ = self.rr("kst", 2)
                kst, ksn = self.kst[sb_], f"kst{sb_}"
                self.rope(tok0, kst[:], [ksn])
                self.dma("sp", self.kb_o[g][:, tok0:tok0 + 512], kst[:], [ksn], ["kb_o"])
                c0 = 4096 + g * 128
                if j == 0:
                    self.dma("sp", bnc[0:128, c0:c0 + 128], kst[:, 0:128], [ksn], ["kv_b"])
                else:
                    self.dma("sp", bnc[128:256, c0:c0 + 128], kst[:, 384:512], [ksn], ["kv_b"])
            for vt in range(5):
                wt, rn = self.wnext()
                for tt in range(4):
                    b = self.rr("acc", 2)
                    ps, psn = self.ps_acc[b], [f"ps_acc{b}"]
                    for k in range(KC):
                        self.mm(ps[:, 0:256], self.hT[:, k, tt * 128:tt * 128 + 128], wt[:, k, :], k == 0, k == KC - 1,
                                [rn, "hT"], psn)
                    sb_ = self.rr("vst", 2)
                    vst, vsn = self.vst[sb_], f"vst{sb_}"
                    self.act(vst[:], ps[:, 0:256], AF.Copy, psn, [vsn])
                    tile_ = 4 * j + tt
                    if vt < 4:
                        self.dma("sp", self.va_o[tile_][:, vt * 256:vt * 256 + 256], vst[:], [vsn], ["va_o"])
                        if tile_ < 2:
                            cc = 2048 + tile_ * 1024 + vt * 256
                            self.dma("sp", bnc[0:128, cc:cc + 256], vst[:], [vsn], ["kv_b"])
                        if tile_ >= 6:
                            cc = 2048 + (tile_ - 6) * 1024 + vt * 256
                            self.dma("sp", bnc[128:256, cc:cc + 256], vst[:], [vsn], ["kv_b"])
                    else:
                        self.dma("sp", self.vb_o[tile_][:, :], vst[:], [vsn], ["vb_o"])
                        if tile_ == 0:
                            self.dma("sp", bnc[0:128, 4352:4608], vst[:], [vsn], ["kv_b"])
                        if tile_ == 7:
                            self.dma("sp", bnc[128:256, 4352:4608], vst[:], [vsn], ["kv_b"])
        self.allgather(bnc, self.kv_g[par], reads=["kv_b"], writes=["kv_g"])

    def load_halo(self, par, s):
        nc = self.nc
        self.tr.op("pool", lambda: nc.gpsimd.indirect_dma_start(
            out=self.halo, out_offset=None, in_=self.kv_g[par],
            in_offset=bass.IndirectOffsetOnAxis(ap=self.idx[:, s:s + 1], axis=0)),
            reads=["kv_g", "idx"], writes=["halo"], dma=True)

    def attn_half(self, li, j, par):
        nc = self.nc
        self.load_halo(par, j)
        self.make_h(1 + 512 * j, 512, V_LN1)
        hrhs = lambda k: self.hT[:, k, 0:512]
        ident, ones = self.cmb[:, 0:128], self.cmb[:, 128:256]
        halo = self.halo
        sbanksA = [(self.ps_s[0], [f"ps_s{i}" for i in range(4)]), (self.ps_s[1], [f"ps_s{4 + i}" for i in range(4)]),
                   (self.ps_r, ["ps_r"])]
        baseA = 0 if j == 0 else 2
        for hp in range(4):
            sl = self.rr("kA", 2)
            kA, vA, bia = self.kA[sl], self.vA[sl], self.bias[sl]
            self.dma("sp", kA, self.ka_o[2 * hp:2 * hp + 2, :, baseA * 128:baseA * 128 + 768].rearrange("h p t -> p h t"),
                     ["ka_o"], [f"kA{sl}"])
            self.dma("sp", vA, self.va_o[baseA:baseA + 6, :, hp * 256:hp * 256 + 256].rearrange("t p f -> p t f"),
                     ["va_o"], [f"vA{sl}"])
            self.dma("pool", bia, self.rpbT_d[li][:, 2 * hp:2 * hp + 2, 0:896], [], [f"bias{sl}"])
            wt, rn = self.wnext()
            for hh in range(2):
                h = 2 * hp + hh
                ps, psn = self.proj_fm(wt, rn, hh * 128, KC, hrhs, ["hT"], 512)
                qs = self.rr("qT", 2)
                qT = self.qT[qs][:, 0, :]
                self.head_norm(ps[:], psn, V_QNA, qT, [f"qT{qs}"], True)
                steps = []
                for bp in range(4):
                    b = 4 * j + bp
                    slots = _a_slots(b)
                    for si, dlt in enumerate(slots):
                        steps.append((bp, b, si, dlt, len(slots)))

                def emit_s(st):
                    bp, b, si, dlt, ns = st
                    kt = b + dlt // 2
                    if kt < 0 or kt > 7:
                        ht = kt + 2 if kt < 0 else kt - 8
                        ktile = halo[:, h * 256 + ht * 128:h * 256 + ht * 128 + 128]
                        vtile = halo[:, 2048 + ht * 1024 + h * 128:2048 + ht * 1024 + h * 128 + 128]
                        kres = ["halo"]
                    else:
                        ktile = kA[:, hh, (kt - baseA) * 128:(kt - baseA) * 128 + 128]
                        vtile = vA[:, kt - baseA, hh * 128:hh * 128 + 128]
                        kres = [f"kA{sl}", f"vA{sl}"]
                    sbank, psn_s = sbanksA[self.rr("ps_sA", 3)]
                    pss = sbank[:, 0:128]
                    self.mm(pss, ktile, qT[:, bp * 128:bp * 128 + 128], True, False, kres + [f"qT{qs}"], psn_s)
                    self.mm(pss, ident, bia[:, hh, (6 - dlt) * 64:(6 - dlt) * 64 + 128], False, False, ["cmb", f"bias{sl}"], psn_s)
                    self.mm(pss, ident, self.negA[:, A_MIDX[(b, dlt)], :], False, True, ["cmb", "negA"], psn_s)
                    pb = self.rr("pT", 4)
                    pT = self.pT[pb][:, 0:128]
                    self.act(pT, pss, AF.Exp, psn_s, [f"pT{pb}"])
                    return (st, pT, pb, vtile, kres)

                def emit_pv(rec, final):
                    (bp, b, si, dlt, ns), pT, pb, vtile, kres = rec
                    last = si == ns - 1
                    self.mm(self.ps_o[:, bp * 128:bp * 128 + 128], vtile, pT, si == 0, last, kres + [f"pT{pb}"], ["ps_o"], sig=False)
                    self.mm(self.ps_d[:, bp * 128:bp * 128 + 128], ones, pT, si == 0, last, ["cmb", f"pT{pb}"], ["ps_d"], sig=final)

                fifo = []
                for st in steps:
                    fifo.append(emit_s(st))
                    if len(fifo) > PIPE:
                        emit_pv(fifo.pop(0), False)
                while fifo:
                    rec = fifo.pop(0)
                    emit_pv(rec, not fifo)
                self.dve(lambda: nc.vector.reciprocal(out=self.t3[:], in_=self.ps_d[:]), ["ps_d"], ["t3"])
                self.dve(lambda h=h: nc.vector.tensor_tensor(out=self.OT[:, h, :], in0=self.ps_o[:], in1=self.t3[:], op=ALU.mult),
                         ["ps_o", "ps_d", "t3"], ["OT"])
        baseB = 0 if j == 0 else 3
        sl = self.rr("kA", 2)
        kB = self.kA[sl][:, :, 0:640]
        vB = self.vA[sl][:, 0:5, :]
        self.dma("sp", kB, self.kb_o[:, :, baseB * 128:baseB * 128 + 640].rearrange("g p t -> p g t"), ["kb_o"], [f"kA{sl}"])
        self.dma("sp", vB, self.vb_o[baseB:baseB + 5, :, :].rearrange("t p f -> p t f"), ["vb_o"], [f"vA{sl}"])
        for g in range(2):
            qs = self.rr("qT", 2)
            qB = self.qT[qs]
            for wp in range(2):
                wt, rn = self.wnext()
                for hh in range(2):
                    ps, psn = self.proj_fm(wt, rn, hh * 128, KC, hrhs, ["hT"], 512)
                    self.head_norm(ps[:], psn, V_QNB, self.qn[:], ["qn"], True)
                    self.rope(512 * j, qB[:, 2 * wp + hh, :], [f"qT{qs}"])
            sbanks = [(self.ps_s[0], [f"ps_s{i}" for i in range(4)]), (self.ps_s[1], [f"ps_s{4 + i}" for i in range(4)]),
                      (self.ps_r, ["ps_r"])]

            def emit_sb(bp, jj):
                b = 4 * j + bp
                kt = b - 1 + jj
                if kt < 0 or kt > 7:
                    ktile = halo[:, 4096 + g * 128:4096 + g * 128 + 128]
                    vtile = halo[:, 4352 + g * 128:4352 + g * 128 + 128]
                    kres = ["halo"]
                else:
                    ktile = kB[:, g, (kt - baseB) * 128:(kt - baseB) * 128 + 128]
                    vtile = vB[:, kt - baseB, g * 128:g * 128 + 128]
                    kres = [f"kA{sl}", f"vA{sl}"]
                mi = None
                if jj == 0:
                    mi = 0 if b == 0 else 1
                elif jj == 2:
                    mi = 3 if b == 7 else 2
                pss, psn_s = sbanks[self.rr("ps_sB", 3)]
                self.mm(pss[:].rearrange("p (h q) -> p h q", h=4), ktile, qB[:, :, bp * 128:bp * 128 + 128], True, mi is None,
                        kres + [f"qT{qs}"], psn_s, sig=(mi is None))
                if mi is not None:
                    for hq in range(4):
                        self.mm(pss[:, hq * 128:hq * 128 + 128], ident, self.negB[:, mi, :], False, hq == 3, ["cmb", "negB"], psn_s,
                                sig=(hq == 3))
                pb = self.rr("pT", 4)
                pT = self.pT[pb]
                self.act(pT[:], pss[:], AF.Exp, psn_s, [f"pT{pb}"])
                return (bp, jj, pT, pb, vtile, kres)

            def emit_pvb(rec):
                bp, jj, pT, pb, vtile, kres = rec
                self.mm(self.ps_o[:], vtile, pT[:], jj == 0, jj == 2, kres + [f"pT{pb}"], ["ps_o"], sig=False)
                self.mm(self.ps_d[:], ones, pT[:], jj == 0, jj == 2, ["cmb", f"pT{pb}"], ["ps_d"], sig=(jj == 2))
                if jj == 2:
                    for hq in range(4):
                        self.dve(lambda hq=hq, g=g: nc.vector.tensor_scalar(
                            out=self.t3[:, hq * 128:hq * 128 + 128], in0=self.ps_d[:, hq * 128:hq * 128 + 128],
                            scalar1=self.sinkexp[:, 4 * g + hq:4 * g + hq + 1], scalar2=None, op0=ALU.add), ["ps_d", "sinkexp"], ["t3"])
                    self.dve(lambda: nc.vector.reciprocal(out=self.t2[:], in_=self.t3[:]), ["t3"], ["t2"])
                    self.dve(lambda g=g, bp=bp: nc.vector.tensor_tensor(
                        out=self.OT[:, 8 + 4 * g:8 + 4 * g + 4, bp * 128:bp * 128 + 128],
                        in0=self.ps_o[:].rearrange("p (h q) -> p h q", h=4), in1=self.t2[:].rearrange("p (h q) -> p h q", h=4), op=ALU.mult),
                        ["ps_o", "ps_d", "t2"], ["OT"])

            fifo = []
            for bp in range(4):
                for jj in range(3):
                    fifo.append(emit_sb(bp, jj))
                    if len(fifo) > PIPE:
                        emit_pvb(fifo.pop(0))
            while fifo:
                emit_pvb(fifo.pop(0))
        if DEBUG and li == 0:
            self.dma("sp", self.dbg_ot[j], self.OT[:], ["OT"], ["dbg_ot"])
            self.dma("sp", self.dbg_halo[j], self.halo, ["halo"], ["dbg_halo"])
        for grp, goff in ((0, V_ONA), (1, V_ONB)):
            self.sumsq([self.OT[:, 8 * grp + c, :] for c in range(8)], 512, ["OT"])
            self.finish_rstd(512, 1.0 / 1024.0, EPS)
            for c in range(8):
                self.dve(lambda c=c, grp=grp, goff=goff: nc.vector.scalar_tensor_tensor(
                    out=self.OT[:, 8 * grp + c, :], in0=self.OT[:, 8 * grp + c, :], scalar=self.vcol(goff + c),
                    in1=self.rstd[:, 0:512], op0=ALU.mult, op1=ALU.mult), ["OT", "rstd", "vecs"], ["OT"])
        for np_ in range(8):
            wt, rn = self.wnext()
            for cc in range(2):
                ch = 2 * np_ + cc
                ps, psn = self.proj_fm(wt, rn, cc * 128, KC, lambda k: self.OT[:, k, :], ["OT"], 512)
                xs = self.x[:, ch, 1 + 512 * j:1 + 512 * j + 512]
                self.dve(lambda ps=ps, xs=xs: nc.vector.tensor_tensor(out=xs, in0=ps[:], in1=xs, op=ALU.add), psn + ["x"], ["x"])

    def xb_exchange(self, par):
        nc = self.nc
        self.dve(lambda: nc.vector.tensor_copy(out=self.xbs[:, 0, :], in_=self.x[:, :, 1]), ["x"], ["xbs"])
        self.dve(lambda: nc.vector.tensor_copy(out=self.xbs[:, 1, :], in_=self.x[:, :, T]), ["x"], ["xbs"])
        self.dma("sp", self.xb_b[par][0:128, :], self.xbs[:, 0, :], ["xbs"], ["xb_b"])
        self.dma("sp", self.xb_b[par][128:256, :], self.xbs[:, 1, :], ["xbs"], ["xb_b"])
        self.allgather(self.xb_b[par], self.xb_g[par], reads=["xb_b"], writes=["xb_g"])
        for s in range(2):
            self.tr.op("pool", lambda s=s: nc.gpsimd.indirect_dma_start(
                out=self.xh[:, s, :], out_offset=None, in_=self.xb_g[par],
                in_offset=bass.IndirectOffsetOnAxis(ap=self.idx[:, s:s + 1], axis=0)),
                reads=["xb_g", "idx"], writes=[f"xh{s}"], dma=True)
        self.dve(lambda: nc.vector.tensor_scalar(out=self.x[:, :, 0], in0=self.xh[:, 0, :], scalar1=self.flag[:, 0:1], scalar2=None, op0=ALU.mult),
                 ["xh0", "flag"], ["xhalo"])
        self.dve(lambda: nc.vector.tensor_scalar(out=self.x[:, :, T + 1], in0=self.xh[:, 1, :], scalar1=self.flag[:, 1:2], scalar2=None, op0=ALU.mult),
                 ["xh1", "flag"], ["xhalo"])

    def ffn_third(self, li, ti):
        nc = self.nc
        a, bnd = TH[ti]
        n = bnd - a
        nu = n + 2
        if ti > 0:
            self.dve(lambda: nc.vector.tensor_copy(out=self.xtmp[:], in_=self.x[:, :, a]), ["x"], ["xtmp"])
            self.dve(lambda: nc.vector.tensor_copy(out=self.x[:, :, a], in_=self.xsv[:, ti - 1, :]), ["xsv"], ["x"])
        self.make_h(a, nu, V_LN2)
        if ti > 0:
            self.dve(lambda: nc.vector.tensor_copy(out=self.x[:, :, a], in_=self.xtmp[:]), ["xtmp"], ["x"])
        for gi, (f0, nf) in enumerate(GROUPS):
            hb = self.rr("hg", 2)
            hg = self.hg[hb]
            for duo in range(nf // 2):
                f = f0 + 2 * duo
                wtg, rng_ = self.wnext()
                wtu, rnu = self.wnext()
                for cc in range(2):
                    fc = f + cc
                    cb = self.rr("cg", 2)
                    cg, cu, sg = self.cg[cb], self.cu[cb], self.sg[cb]
                    for (wt, rn, dst, dn, ch) in ((wtg, rng_, cg, f"cg{cb}", fc), (wtu, rnu, cu, f"cu{cb}", FC + fc)):
                        ps, psn = self.proj_fm(wt, rn, cc * 128, KC, lambda k: self.hT[:, k, 0:nu], ["hT"], nu, wide=True)
                        w0 = self.vcol(V_CW + ch)
                        w1 = self.vcol(V_CW + 88 + ch)
                        w2 = self.vcol(V_CW + 176 + ch)
                        bb = self.vcol(V_CB + ch)
                        self.act(dst[:, 0:n], ps[:, 1:n + 1], AF.Identity, psn + ["vecs"], [dn], scale=w1, bias=bb)
                        self.dve(lambda ps=ps, dst=dst, w0=w0: nc.vector.scalar_tensor_tensor(
                            out=dst[:, 0:n], in0=ps[:, 0:n], scalar=w0, in1=dst[:, 0:n], op0=ALU.mult, op1=ALU.add),
                            psn + ["vecs", dn], [dn])
                        self.dve(lambda ps=ps, dst=dst, w2=w2: nc.vector.scalar_tensor_tensor(
                            out=dst[:, 0:n], in0=ps[:, 2:n + 2], scalar=w2, in1=dst[:, 0:n], op0=ALU.mult, op1=ALU.add),
                            psn + ["vecs", dn], [dn])
                    self.act(sg[:, 0:n], cg[:, 0:n], AF.Silu, [f"cg{cb}"], [f"sg{cb}"])
                    self.dve(lambda hg=hg, sg=sg, cu=cu, k=2 * duo + cc: nc.vector.tensor_tensor(
                        out=hg[:, k, 0:n], in0=sg[:, 0:n], in1=cu[:, 0:n], op=ALU.mult), [f"sg{cb}", f"cu{cb}"], [f"hg{hb}"])
            if DEBUG and li == 0 and ti == 0 and gi == 0:
                self.dma("sp", self.dbg_hg, hg, [f"hg{hb}"], ["dbg_hg"])
                self.dma("sp", self.dbg_h2, self.hT[:], ["hT"], ["dbg_h2"])
            for np_ in range(8):
                wt, rn = self.wnext()
                for cc in range(2):
                    ch = 2 * np_ + cc
                    ps, psn = self.proj_fm(wt, rn, cc * 128, nf, lambda k: hg[:, k, 0:n], [f"hg{hb}"], n, wide=True)
                    xs = self.x[:, ch, 1 + a:1 + a + n]
                    self.dve(lambda ps=ps, xs=xs: nc.vector.tensor_tensor(out=xs, in0=ps[:, 0:n], in1=xs, op=ALU.add), psn + ["x"], ["x"])

    def build(self):
        L = self.L
        tr = self.tr
        for li in range(L):
            self.plan += self.plan_layer(li)
        self.gather_weight("in", 0, 0)
        self.prologue()
        for li in range(L):
            if li > 0:
                tr.new_epoch()
            par = li % 2
            self.dma("sp", self.vecs[:], self.vecs_d[li], [], ["vecs"])
            self.act(self.sinkexp[:], self.vcol(V_SINK, 8), AF.Exp, ["vecs"], ["sinkexp"])
            tr.alias(FFN_NAMES, ATT_NAMES)
            self.kv_phase(li, par)
            if li == 0:
                self.gather_weight("out", 0, 0)
            for j in range(2):
                self.attn_half(li, j, par)
            self.xb_exchange(par)
            nxt = ([0] if li == 0 else []) + ([li + 1] if li + 1 < L else [])
            for l2 in nxt:
                kinds = [("up", 0), ("up", 1), ("up", 2), ("dn", 0), ("dn", 1), ("dn", 2)]
                if l2 != 0 or li != 0:
                    kinds = [("in", 0), ("out", 0)] + kinds
                for kind, q in kinds:
                    self.gather_weight(kind, l2, q)
            tr.alias(ATT_NAMES, FFN_NAMES)
            if DEBUG and li == 0:
                self.dma("sp", self.dbg_xmid, self.x[:], ["x", "xhalo"], ["dbg_xmid"])
            for k_, (a_, _) in enumerate(TH[1:]):
                self.dve(lambda k_=k_, a_=a_: self.nc.vector.tensor_copy(out=self.xsv[:, k_, :], in_=self.x[:, :, a_]), ["x"], ["xsv"])
            for ti in range(3):
                self.ffn_third(li, ti)
        assert self.plan_i == len(self.plan), (self.plan_i, len(self.plan))
        self.dma("sp", self.yT_d.rearrange("(k p) t -> p k t", p=128), self.x[:, :, 1:T + 1], ["x"], ["yT"])
        tr.wait_all("sp", ["yT"])
        for e in ("sp", "pool"):
            for sl in tr.rots[e]:
                if sl.cnt:
                    tr._wait(e, Tok(sl.sem, sl.cnt, "dma", sl.name))
        self.nc.all_engine_barrier()
        self.nc.clear_and_free_semaphores(tr.all_sems + [self.cc_sem])
        self.nc.all_engine_barrier()
        return self.nc


_PROG_CACHE = {}


def _get_prog(n_layers):
    if n_layers not in _PROG_CACHE:
        p = Prog(n_layers)
        p.build()
        _PROG_CACHE[n_layers] = p
    return _PROG_CACHE[n_layers]


def _colshard(w, c0, n):
    out = np.zeros(w.shape[:-1] + (n,), np.float32)
    hi = min(w.shape[-1], c0 + n)
    if hi > c0:
        out[..., 0:hi - c0] = w[..., c0:hi]
    return out


def _tag(a, c):
    b = a.copy()
    b[..., 896] = float(c)
    return b


def _in_maps(xT_shards, inp, layers):
    vecs = _vecs(inp, layers)
    cmat = _cmat()
    rpbT = np.zeros((len(layers), 128, 8, 897), np.float32)
    for i_, l in enumerate(layers):
        rpbT[i_, :, :, 0:896] = _rpb_table(inp["rpb"][l])
    maps = []
    p = np.arange(128)
    for c in range(NC):
        idx = np.stack([max(c - 1, 0) * 256 + 128 + p, min(c + 1, NC - 1) * 256 + p], axis=1).astype(np.int32)
        flag = np.ones((128, 2), np.float32)
        if c == 0:
            flag[:, 0] = 0.0
        if c == NC - 1:
            flag[:, 1] = 0.0
        maps.append({
            "xT": xT_shards[c],
            "pos": np.ascontiguousarray(inp["positions"][c * T:(c + 1) * T]).astype(np.int32),
            "vecs": vecs, "cmat": cmat, "negA": _neg_a(c), "negB": _neg_b(c), "rpbT": _tag(rpbT, c), "idx": idx, "flag": flag,
            "w_in": _colshard(inp["w_in"][layers], c * 768, 768),
            "w_out": _colshard(inp["w_out"][layers], c * 256, 256),
            "w_up": _colshard(inp["w_up"][layers], c * 1536, 1536),
            "w_down": _colshard(inp["w_down"][layers], c * 256, 256),
        })
    return maps


LAYERS_PER_LAUNCH = 4
PIPE = 2
DEBUG = False


def kernel(**inputs):
    inp = {k: np.asarray(v) for k, v in inputs.items()}
    x = inp["x"]
    depth = inp["w_in"].shape[0]
    xT = [np.ascontiguousarray(x[0, c * T:(c + 1) * T, :].T) for c in range(NC)]
    l0 = 0
    while l0 < depth:
        layers = list(range(l0, min(depth, l0 + LAYERS_PER_LAUNCH)))
        prog = _get_prog(len(layers))
        res = run_bass_kernel_spmd(prog.nc, _in_maps(xT, inp, layers), core_ids=list(range(NC)))
        xT = [np.ascontiguousarray(res.results[c]["yT"]) for c in range(NC)]
        l0 += len(layers)
    out = np.concatenate([xT[c].T for c in range(NC)], axis=0)[None]
    return np.ascontiguousarray(out).astype(np.float32)
```
